# Optimizing a Trainium2 kernel written in Bass

```python
import jax, jax.numpy as jnp
from jax import lax
import numpy as np

D_MODEL = 1024
BATCH = 4
SEQ = 4096
DEPTH = 1

N_META = 16
EPS = 1e-6
D_FF = 2816
CHUNK = 64
A_DK = 128
A_DV = 128
A_HEADS = D_MODEL // A_DV
A_CONV = 4
A_WK = A_HEADS * A_DK
A_WV = A_HEADS * A_DV
B_N = 64
B_HEADS = D_MODEL // B_N
B_W = B_HEADS * B_N
W_LORA = 64
AA_LORA = 64
G_LORA = 160
B_GN_EPS = B_N * 1e-5
B_COLS = 3 * B_W + W_LORA + AA_LORA + G_LORA
IN_SIZES = (A_WK, A_WK, A_WV, A_WV, A_HEADS, A_HEADS, B_COLS, D_MODEL, D_MODEL)
IN_TOTAL = sum(IN_SIZES)

kernel_name = "meta_macaron_deltanet_rwkv7_hybrid"


def _offsets(sizes):
    out, acc = [], 0
    for s in sizes[:-1]:
        acc += s
        out.append(acc)
    return out


def _rmsnorm(x, gain):
    xf = x.astype(jnp.float32)
    y = xf * lax.rsqrt(jnp.mean(xf * xf, axis=-1, keepdims=True) + EPS)
    return (y * gain.astype(jnp.float32)).astype(x.dtype)


def _l2norm(x):
    xf = x.astype(jnp.float32)
    return xf * lax.rsqrt(jnp.sum(xf * xf, axis=-1, keepdims=True) + 1e-6)


def _swiglu(x, w_gu, w_down):
    gate, up = jnp.split(x @ w_gu, 2, axis=-1)
    return (jax.nn.silu(gate) * up) @ w_down


def _causal_dwconv(x, w):
    k = w.shape[0]
    return lax.conv_general_dilated(x, w[:, None, :].astype(x.dtype), window_strides=(1,),
                                    padding=[(k - 1, 0)], dimension_numbers=('NWC', 'WIO', 'NWC'),
                                    feature_group_count=x.shape[-1])


def _gated_delta_chunked(q, k, v, beta, g):
    b, h, t, dk = q.shape
    dv = v.shape[-1]
    n = t // CHUNK

    def ch(z):
        return z.reshape((b, h, n, CHUNK) + z.shape[3:])

    q, k, v, beta, g = ch(q), ch(k), ch(v), ch(beta), ch(g)
    g = jnp.cumsum(g, axis=-1)
    kb = k * beta[..., None]
    vb = v * beta[..., None]
    idx = jnp.arange(CHUNK)
    incl = idx[:, None] >= idx[None, :]
    strict = idx[:, None] > idx[None, :]
    diff = g[..., :, None] - g[..., None, :]
    decay = jnp.where(incl, jnp.exp(jnp.where(incl, diff, 0.0)), 0.0)
    m = jnp.where(strict, jnp.einsum('bhncd,bhnsd->bhncs', kb, k) * decay, 0.0)
    eye = jnp.eye(CHUNK, dtype=m.dtype)
    tinv = lax.linalg.triangular_solve(m + eye, jnp.broadcast_to(eye, m.shape), left_side=True,
                                       lower=True, unit_diagonal=True)
    u = jnp.einsum('bhncs,bhnsd->bhncd', tinv, vb)
    wk = jnp.einsum('bhncs,bhnsd->bhncd', tinv, kb * jnp.exp(g)[..., None])
    attn = jnp.einsum('bhncd,bhnsd->bhncs', q, k) * decay
    qg = q * jnp.exp(g)[..., None]
    g_last = g[..., -1]
    k_tail = k * jnp.exp(g_last[..., None] - g)[..., None]

    def step(state, inp):
        u_i, w_i, attn_i, qg_i, kt_i, gl_i = inp
        v_new = u_i - jnp.einsum('bhcd,bhde->bhce', w_i, state)
        o = jnp.einsum('bhcd,bhde->bhce', qg_i, state) + jnp.einsum('bhcs,bhse->bhce', attn_i, v_new)
        state = state * jnp.exp(gl_i)[..., None, None] + jnp.einsum('bhcd,bhce->bhde', kt_i, v_new)
        return state, o

    xs = (jnp.moveaxis(u, 2, 0), jnp.moveaxis(wk, 2, 0), jnp.moveaxis(attn, 2, 0),
          jnp.moveaxis(qg, 2, 0), jnp.moveaxis(k_tail, 2, 0), jnp.moveaxis(g_last, 2, 0))
    s0 = jnp.zeros((b, h, dk, dv), jnp.float32)
    _, o = lax.scan(step, s0, xs)
    return jnp.moveaxis(o, 0, 2).reshape(b, h, t, dv)


def _deltanet_branch(q, k, v, z, beta_pre, alpha_pre, conv_w, log_rate, dt_bias, out_gain):
    bsz, t, _ = q.shape
    qkv = jax.nn.silu(_causal_dwconv(jnp.concatenate([q, k, v], axis=-1), conv_w))
    q, k, v = jnp.split(qkv, [A_WK, 2 * A_WK], axis=-1)
    q = _l2norm(q.reshape(bsz, t, A_HEADS, A_DK)) * (A_DK ** -0.5)
    k = _l2norm(k.reshape(bsz, t, A_HEADS, A_DK))
    v = v.reshape(bsz, t, A_HEADS, A_DV).astype(jnp.float32)
    beta = jax.nn.sigmoid(beta_pre.astype(jnp.float32))
    g = -jnp.exp(log_rate.astype(jnp.float32)) * jax.nn.softplus(
        alpha_pre.astype(jnp.float32) + dt_bias.astype(jnp.float32))
    pad = CHUNK - N_META

    def prep(a):
        a = jnp.pad(a, ((0, 0), (pad, 0)) + ((0, 0),) * (a.ndim - 2))
        return jnp.moveaxis(a, 2, 1)

    o = _gated_delta_chunked(prep(q), prep(k), prep(v), prep(beta), prep(g))
    o = jnp.moveaxis(o, 1, 2)[:, pad:]
    o = o * lax.rsqrt(jnp.mean(o * o, axis=-1, keepdims=True) + EPS) * out_gain.astype(jnp.float32)
    o = o.reshape(bsz, t, A_WV) * jax.nn.silu(z.astype(jnp.float32))
    return o.astype(z.dtype)


def _rwkv7_scan(r, w, k, v, a_vec, b_vec):
    def step(state, inp):
        r_t, w_t, k_t, v_t, a_t, b_t = inp
        sa = jnp.einsum('bhvk,bhk->bhv', state, a_t)
        state = state * w_t[:, :, None, :] + sa[..., None] * b_t[:, :, None, :] \
            + v_t[..., None] * k_t[:, :, None, :]
        return state, jnp.einsum('bhvk,bhk->bhv', state, r_t)

    bsz, t, h, n = r.shape
    xs = tuple(jnp.moveaxis(a, 1, 0) for a in (r, w, k, v, a_vec, b_vec))
    s0 = jnp.zeros((bsz, h, n, n), jnp.float32)
    _, y = lax.scan(step, s0, xs)
    return jnp.moveaxis(y, 0, 1)


def _rwkv7_branch(zb, mu, w0, w_up, a0, a_up, g_up, k_k, k_a, r_k, ln_gain, ln_bias):
    bsz, t, _ = zb.shape
    f32 = jnp.float32
    zf = zb.astype(f32)
    prev = jnp.pad(zf, ((0, 0), (1, 0), (0, 0)))[:, :-1]
    zf = zf + (prev - zf) * mu.astype(f32)
    r, k, v, wd, ad, gd = jnp.split(
        zf, [B_W, 2 * B_W, 3 * B_W, 3 * B_W + W_LORA, 3 * B_W + W_LORA + AA_LORA], axis=-1)
    w_log = -jax.nn.softplus(-(w0.astype(f32) + jnp.tanh(wd) @ w_up.astype(f32))) - 0.5
    decay = jnp.exp(-jnp.exp(w_log))
    a = jax.nn.sigmoid(a0.astype(f32) + ad @ a_up.astype(f32))
    gate = jax.nn.sigmoid(gd) @ g_up.astype(f32)
    hs = (bsz, t, B_HEADS, B_N)
    kk = _l2norm((k * k_k.astype(f32)).reshape(hs))
    k = k * (1.0 + (a - 1.0) * k_a.astype(f32))
    r, k, v, decay, a = r.reshape(hs), k.reshape(hs), v.reshape(hs), decay.reshape(hs), a.reshape(hs)
    y = _rwkv7_scan(r, decay, k, v, -kk, kk * a)
    mean = jnp.mean(y, axis=-1, keepdims=True)
    var = jnp.mean(jnp.square(y - mean), axis=-1, keepdims=True)
    y = (y - mean) * lax.rsqrt(var + B_GN_EPS) * ln_gain.astype(f32).reshape(B_HEADS, B_N) \
        + ln_bias.astype(f32).reshape(B_HEADS, B_N)
    y = y + jnp.sum(r * k * r_k.astype(f32), axis=-1, keepdims=True) * v
    return (y.reshape(bsz, t, B_W) * gate).astype(zb.dtype)


def setup_inputs(seed: int = 0) -> dict:
    key = jax.random.key(seed)
    ks = jax.random.split(key, 32)
    f32 = jnp.float32

    def nrm(k, shape, scale):
        return jax.random.normal(k, shape, f32) * scale

    def gain(k, shape):
        return 1.0 + 0.02 * jax.random.normal(k, shape, f32)

    dt = jnp.exp(jax.random.uniform(ks[8], (DEPTH, A_HEADS), f32, np.log(1e-3), np.log(1e-1)))
    return {
        "x": nrm(ks[0], (BATCH, SEQ, D_MODEL), 1.0),
        "meta_tokens": nrm(ks[1], (N_META, D_MODEL), 1.0),
        "ffn1_norm": gain(ks[2], (DEPTH, D_MODEL)),
        "ffn1_w_gu": nrm(ks[3], (DEPTH, D_MODEL, 2 * D_FF), D_MODEL ** -0.5),
        "ffn1_w_down": nrm(ks[4], (DEPTH, D_FF, D_MODEL), D_FF ** -0.5),
        "mix_norm": gain(ks[5], (DEPTH, D_MODEL)),
        "w_in": nrm(ks[6], (DEPTH, D_MODEL, IN_TOTAL), D_MODEL ** -0.5),
        "a_conv_w": nrm(ks[7], (DEPTH, A_CONV, 2 * A_WK + A_WV), A_CONV ** -0.5),
        "a_log_rate": jnp.log(jax.random.uniform(ks[9], (DEPTH, A_HEADS), f32, 1.0, 16.0)),
        "a_dt_bias": dt + jnp.log(-jnp.expm1(-dt)),
        "a_out_norm": gain(ks[10], (DEPTH, A_DV)),
        "b_shift_mu": jax.random.uniform(ks[11], (DEPTH, B_COLS), f32, 0.0, 1.0),
        "b_w0": jax.random.uniform(ks[12], (DEPTH, B_W), f32, -6.5, -1.5),
        "b_w_up": nrm(ks[13], (DEPTH, W_LORA, B_W), 0.5 * W_LORA ** -0.5),
        "b_a0": nrm(ks[14], (DEPTH, B_W), 0.1),
        "b_a_up": nrm(ks[15], (DEPTH, AA_LORA, B_W), AA_LORA ** -0.5),
        "b_g_up": nrm(ks[16], (DEPTH, G_LORA, B_W), G_LORA ** -0.5),
        "b_k_k": 0.85 + 0.05 * jax.random.normal(ks[17], (DEPTH, B_W), f32),
        "b_k_a": 1.0 + 0.05 * jax.random.normal(ks[18], (DEPTH, B_W), f32),
        "b_r_k": nrm(ks[19], (DEPTH, B_HEADS, B_N), 0.1),
        "b_ln_gain": gain(ks[20], (DEPTH, B_W)),
        "b_ln_bias": nrm(ks[21], (DEPTH, B_W), 0.02),
        "w_out": nrm(ks[22], (DEPTH, D_MODEL, D_MODEL), D_MODEL ** -0.5),
        "ffn2_norm": gain(ks[23], (DEPTH, D_MODEL)),
        "ffn2_w_gu": nrm(ks[24], (DEPTH, D_MODEL, 2 * D_FF), D_MODEL ** -0.5),
        "ffn2_w_down": nrm(ks[25], (DEPTH, D_FF, D_MODEL), D_FF ** -0.5),
        "final_norm": gain(ks[26], (D_MODEL,)),
    }


def reference(x, meta_tokens, ffn1_norm, ffn1_w_gu, ffn1_w_down, mix_norm, w_in, a_conv_w,
              a_log_rate, a_dt_bias, a_out_norm, b_shift_mu, b_w0, b_w_up, b_a0, b_a_up, b_g_up,
              b_k_k, b_k_a, b_r_k, b_ln_gain, b_ln_bias, w_out, ffn2_norm, ffn2_w_gu, ffn2_w_down,
              final_norm):
    bsz = x.shape[0]
    meta = jnp.broadcast_to(meta_tokens[None].astype(x.dtype), (bsz, N_META, D_MODEL))
    h = jnp.concatenate([meta, x], axis=1)
    for l in range(DEPTH):
        h = h + 0.5 * _swiglu(_rmsnorm(h, ffn1_norm[l]), ffn1_w_gu[l], ffn1_w_down[l])
        u = _rmsnorm(h, mix_norm[l])
        aq, ak, av, az, abeta, aalpha, bcols, ga, gb = jnp.split(u @ w_in[l], _offsets(IN_SIZES), axis=-1)
        o_a = _deltanet_branch(aq, ak, av, az, abeta, aalpha, a_conv_w[l], a_log_rate[l],
                               a_dt_bias[l], a_out_norm[l])
        o_b = _rwkv7_branch(bcols, b_shift_mu[l], b_w0[l], b_w_up[l], b_a0[l], b_a_up[l], b_g_up[l],
                            b_k_k[l], b_k_a[l], b_r_k[l], b_ln_gain[l], b_ln_bias[l])
        merged = jax.nn.sigmoid(ga) * o_a + jax.nn.sigmoid(gb) * o_b
        h = h + merged @ w_out[l]
        h = h + 0.5 * _swiglu(_rmsnorm(h, ffn2_norm[l]), ffn2_w_gu[l], ffn2_w_down[l])
    return _rmsnorm(h, final_norm)[:, N_META:]
```

```python
import contextlib
import numpy as np
import concourse.bass as bass
import concourse.mybir as mybir

F32 = mybir.dt.float32
BF16 = mybir.dt.bfloat16
ALU = mybir.AluOpType
AF = mybir.ActivationFunctionType
ENGS = ['pe', 'act', 'dve', 'pool', 'sp']
EPOCH = 2000
NSLOT = 8


class Tile:
    __slots__ = ('ap', 'w', 'rs', 'name')

    def __init__(self, ap, name=''):
        self.ap = ap
        self.w = None
        self.rs = []
        self.name = name

    def __getitem__(self, k):
        return self.ap[k]


class Op:
    __slots__ = ('eng', 'fn', 'waits', 'dwaits', 'idx', 'signal', 'isdma', 'slot', 'semval',
                 'clock', 'signo', 'inc', 'noop')


class Prog:
    def __init__(self, nc):
        self.nc = nc
        self.ops = {e: [] for e in ENGS}
        self.known = {e: {f: -1 for f in ENGS} for e in ENGS}
        self.kdma = {e: {} for e in ENGS}
        self.slots = {q: [[None, 0, None] for _ in range(NSLOT)] for q in ('sp', 'pool')}
        self.slot_rr = {'sp': 0, 'pool': 0}
        self.ccs = []

    def op(self, eng, fn, reads=(), writes=(), dma=False, cc=False, noop=False):
        o = Op()
        o.noop = noop
        o.eng = eng
        o.fn = fn
        o.idx = len(self.ops[eng])
        o.signal = False
        o.isdma = dma or cc
        o.slot = None
        o.semval = 0
        o.inc = 16
        deps = []
        rawset = set()
        for t in reads:
            if t.w is not None:
                deps.append(t.w)
                rawset.add(id(t.w))
        for t in writes:
            if t.w is not None:
                deps.append(t.w)
            deps.extend(t.rs)
        waits = {}
        dwaits = []
        if o.isdma:
            if cc:
                slot = ('cc', len(self.ccs))
                self.ccs.append(o)
                o.slot = slot
                o.semval = 1
                o.inc = 1
            else:
                k = self.slot_rr[eng]
                self.slot_rr[eng] = (k + 1) % NSLOT
                sl = self.slots[eng][k]
                prev = sl[2]
                if prev is not None:
                    deps.append(prev)
                sl[1] += 1
                sl[2] = o
                o.slot = (eng, k)
                o.semval = 16 * sl[1]
        seen = set()
        for d in deps:
            if d is o or id(d) in seen:
                continue
            seen.add(id(d))
            if d.isdma:
                if self.kdma[eng].get(d.slot, 0) < d.semval:
                    dwaits.append(d)
                    self.kdma[eng][d.slot] = d.semval
            else:
                if d.eng == eng:
                    if eng == 'pe' or id(d) not in rawset:
                        continue
                if self.known[eng][d.eng] < d.idx:
                    waits[d.eng] = max(waits.get(d.eng, -1), d.idx)
        for f, j in waits.items():
            dop = self.ops[f][j]
            dop.signal = True
            for g, v in dop.clock.items():
                if self.known[eng][g] < v:
                    self.known[eng][g] = v
            if self.known[eng][f] < j:
                self.known[eng][f] = j
        o.waits = list(waits.items())
        o.dwaits = dwaits
        ck = dict(self.known[eng])
        if not o.isdma:
            ck[eng] = o.idx
        o.clock = ck
        for t in reads:
            t.rs.append(o)
        for t in writes:
            t.w = o
            t.rs = []
        self.ops[eng].append(o)
        return o

    def barrier(self):
        lasts = {e: (self.ops[e][-1] if self.ops[e] else None) for e in ENGS}
        dummy = Tile(None, 'bar')
        for e in ENGS:
            cnt = 0
            for o in reversed(self.ops[e]):
                if o.noop:
                    if e in ('sp', 'pool'):
                        break
                    continue
                dummy.rs.append(o)
                cnt += 1
                if (not o.isdma) or cnt >= 2 * NSLOT + 4:
                    break
        rs = list(dummy.rs)
        for e in ENGS:
            dummy.rs = list(rs)
            dummy.w = None
            self.op(e, lambda eng: None, writes=[dummy], noop=True)

    def finalize(self, block, sem_alloc):
        nc = self.nc
        nsig = {}
        for e in ENGS:
            n = 0
            for o in self.ops[e]:
                if o.signal:
                    o.signo = n
                    n += 1
            nsig[e] = n
        esems = {e: [sem_alloc(f"s_{e}_{i}") for i in range((nsig[e] + EPOCH - 1) // EPOCH + 1)] for e in ENGS}
        ssems = {}
        for q in ('sp', 'pool'):
            for k in range(NSLOT):
                ssems[(q, k)] = sem_alloc(f"d_{q}_{k}")
        for i in range(len(self.ccs)):
            ssems[('cc', i)] = sem_alloc(f"cc_{i}")
        ops = self.ops

        def emit(e, engname):
            for o in ops[engname]:
                for f, j in o.waits:
                    n = ops[f][j].signo
                    e.wait_ge(esems[f][n // EPOCH], n % EPOCH + 1)
                for d in o.dwaits:
                    e.wait_ge(ssems[d.slot], d.semval)
                ins = o.fn(e)
                if ins is None:
                    assert not o.signal and not o.isdma, (engname, o.idx)
                    continue
                if o.isdma:
                    if o.inc == 1:
                        ins.then_inc(ssems[o.slot])
                    else:
                        ins.then_inc(ssems[o.slot], 16)
                elif o.signal:
                    ins.then_inc(esems[engname][o.signo // EPOCH], 1)

        @block.tensor
        def _(e):
            emit(e, 'pe')

        @block.scalar
        def _(e):
            emit(e, 'act')

        @block.vector
        def _(e):
            emit(e, 'dve')

        @block.gpsimd
        def _(e):
            emit(e, 'pool')

        @block.sync
        def _(e):
            emit(e, 'sp')
        return {e: len(ops[e]) for e in ENGS}, nsig


D = 1024
DFF = 2816
NFC = 22
TH = 2112
TP = 4224
NT = 352
NST = 2
STW = TH // NST
EPS = 1e-6


class Arena:
    def __init__(self, nc, name, nbytes):
        self.t = nc.alloc_sbuf_tensor(name, [128, nbytes // 4], F32)
        self.cap = nbytes // 4
        self.off = 0
        self.peak = 0

    def reset(self, off=0):
        self.off = off

    def alloc(self, free_shape, dtype=F32, name=''):
        n = int(np.prod(free_shape))
        esz = 4 if dtype == F32 else 2
        n4 = (n * esz + 3) // 4
        n4 = (n4 + 7) // 8 * 8
        assert self.off + n4 <= self.cap, (name, self.off, n4, self.cap)
        ap = self.t[:, self.off:self.off + n4]
        self.off += n4
        self.peak = max(self.peak, self.off)
        if dtype != F32:
            ap = ap.bitcast(dtype)
        ap = ap[:, 0:n]
        if len(free_shape) == 2:
            ap = ap.rearrange("p (a b) -> p a b", a=free_shape[0])
        elif len(free_shape) == 3:
            ap = ap.rearrange("p (a b c) -> p a b c", a=free_shape[0], b=free_shape[1])
        return Tile(ap, name)


class Pool:
    def __init__(self, tiles):
        self.tiles = tiles
        self.i = 0

    def get(self):
        t = self.tiles[self.i]
        self.i = (self.i + 1) % len(self.tiles)
        return t


class K:
    pass


def rmsnorm_tile(P, k, hT, gain_col0, out_t, out_sl, tok_sl, n):
    sq = k.sq_pool.get()
    P.op('act', lambda e: e.activation(sq[:, :, 0:n], hT[:, :, tok_sl], AF.Square), reads=[hT], writes=[sq])
    ps = k.psum.get()
    for dc in range(8):
        P.op('pe', lambda e, dc=dc: e.matmul(ps[:, 0:n], k.ones_bf[:, :], sq[:, dc, 0:n], start=(dc == 0), stop=(dc == 7)),
             reads=[sq, k.ones_bf], writes=[ps])
    rstd = k.rstd_pool.get()
    P.op('act', lambda e: e.activation(rstd[:, 0:n], ps[:, 0:n], AF.Sqrt, bias=k.epsc[:, 1:2]), reads=[ps, k.epsc], writes=[rstd])
    P.op('dve', lambda e: e.reciprocal(rstd[:, 0:n], rstd[:, 0:n]), reads=[rstd], writes=[rstd])
    for dc in range(8):
        if dc % 2 == 0:
            P.op('dve', lambda e, dc=dc: e.scalar_tensor_tensor(out_t[:, dc, out_sl], hT[:, dc, tok_sl],
                                                                k.par[:, gain_col0 + dc:gain_col0 + dc + 1],
                                                                rstd[:, 0:n], ALU.mult, ALU.mult),
                 reads=[hT, rstd, k.par], writes=[out_t])
        else:
            tmp = k.ntmp_pool.get()
            P.op('pool', lambda e, dc=dc, tmp=tmp: e.tensor_scalar(tmp[:, 0:n], hT[:, dc, tok_sl],
                                                                   k.par[:, gain_col0 + dc:gain_col0 + dc + 1], None, ALU.mult),
                 reads=[hT, k.par], writes=[tmp])
            P.op('pool', lambda e, dc=dc, tmp=tmp: e.tensor_tensor(out_t[:, dc, out_sl], tmp[:, 0:n], rstd[:, 0:n], ALU.mult),
                 reads=[tmp, rstd], writes=[out_t])


def ffn(P, k, hT, wgu_d, wd_d, gain_col0):
    nt_per = STW // NT
    for st in range(NST):
        hn = k.hn
        for j in range(nt_per):
            t0 = st * STW + j * NT
            rmsnorm_tile(P, k, hT, gain_col0, hn, slice(j * NT, (j + 1) * NT), slice(t0, t0 + NT), NT)
        for fc in range(NFC):
            wg = k.wgu_pool.get()
            P.op('pool', lambda e, wg=wg, fc=fc: e.dma_start(out=wg[:, :, :], in_=wgu_d[fc]), writes=[wg], dma=True)
            wdn = k.wd_pool.tiles[fc]
            if st == 0:
                P.op('pool', lambda e, wdn=wdn, fc=fc: e.dma_start(out=wdn[:, :], in_=wd_d[fc]), writes=[wdn], dma=True)
            for j in range(nt_per):
                tsl = slice(j * NT, (j + 1) * NT)
                pg = k.psum.get()
                for dc in range(8):
                    P.op('pe', lambda e, dc=dc, pg=pg, wg=wg, tsl=tsl: e.matmul(pg[:, 0:NT], wg[:, dc, 0:128], hn[:, dc, tsl], start=(dc == 0), stop=(dc == 7)),
                         reads=[wg, hn], writes=[pg])
                pu = k.psum.get()
                for dc in range(8):
                    P.op('pe', lambda e, dc=dc, pu=pu, wg=wg, tsl=tsl: e.matmul(pu[:, 0:NT], wg[:, dc, 128:256], hn[:, dc, tsl], start=(dc == 0), stop=(dc == 7)),
                         reads=[wg, hn], writes=[pu])
                sg = k.sg_pool.get()
                P.op('act', lambda e, sg=sg, pg=pg: e.activation(sg[:, 0:NT], pg[:, 0:NT], AF.Silu), reads=[pg], writes=[sg])
                at = k.act_tiles[fc]
                P.op('dve', lambda e, sg=sg, pu=pu, at=at, tsl=tsl: e.tensor_tensor(at[:, tsl], sg[:, 0:NT], pu[:, 0:NT], ALU.mult),
                     reads=[sg, pu], writes=[at])
            k.wd_res[fc] = wdn
        for j in range(nt_per):
            tsl = slice(j * NT, (j + 1) * NT)
            t0 = st * STW + j * NT
            for dco in range(8):
                po = k.psum.get()
                for fc in range(NFC):
                    P.op('pe', lambda e, fc=fc, po=po, dco=dco, tsl=tsl: e.matmul(po[:, 0:NT], k.wd_res[fc][:, dco * 128:(dco + 1) * 128], k.act_tiles[fc][:, tsl],
                                                                                   start=(fc == 0), stop=(fc == NFC - 1)),
                         reads=[k.wd_res[fc], k.act_tiles[fc]], writes=[po])
                P.op('dve', lambda e, po=po, dco=dco, t0=t0: e.scalar_tensor_tensor(hT[:, dco, t0:t0 + NT], po[:, 0:NT], 0.5, hT[:, dco, t0:t0 + NT], ALU.mult, ALU.add),
                     reads=[po, hT], writes=[hT])


C = 128
NCH = TP // C
HALO = 3
NA = 2568
NB = 2336
GN_EPS = 64 * 1e-5
PC_FFN1, PC_MIX, PC_FFN2, PC_FIN = 0, 8, 16, 24
PC_CONV = 32
PC_OG = 80
PC_MU = 81
PC_OMM = 93
PC_W0H = 105
PC_A0H = 109
PC_KK = 113
PC_KA = 117
PC_RK = 121
PC_LNG = 125
PC_LNB = 129
PC_MUL = 133
PC_OMML = 137
PC_W0, PC_A0 = 141, 145
PC_OGH = 149
PC_F, PC_1MF = 150, 151
NPAR = 160
CC_ID, CC_SL, CC_SU, CC_IU, CC_ONE, CC_BLK = 0, 128, 256, 384, 512, 640
NCONST = 768


def MM(P, o, oap, l, lap, r, rap, st=True, sp=True):
    P.op('pe', lambda e: e.matmul(oap, lap, rap, start=st, stop=sp), reads=[l, r], writes=[o])


def TT(P, eng, o, oap, a, aap, b, bap, op):
    P.op(eng, lambda e: e.tensor_tensor(oap, aap, bap, op), reads=[a, b], writes=[o])


def TS(P, eng, o, oap, a, aap, s1, s2, op0, op1=None, rd=()):
    if op1 is None:
        P.op(eng, lambda e: e.tensor_scalar(oap, aap, s1, None, op0), reads=[a] + list(rd), writes=[o])
    else:
        P.op(eng, lambda e: e.tensor_scalar(oap, aap, s1, s2, op0, op1), reads=[a] + list(rd), writes=[o])


def STT(P, o, oap, a, aap, sc, b, bap, op0, op1, rd=()):
    P.op('dve', lambda e: e.scalar_tensor_tensor(oap, aap, sc, bap, op0, op1), reads=[a, b] + list(rd), writes=[o])


def ACTF(P, o, oap, a, aap, func, bias=None, scale=None, rd=()):
    kw = {}
    if bias is not None:
        kw['bias'] = bias
    if scale is not None:
        kw['scale'] = scale
    P.op('act', lambda e: e.activation(oap, aap, func, **kw), reads=[a] + list(rd), writes=[o])


def tri_inverse(P, k, NT0, N0):
    cst = k.cst
    X = k.x_pool.get()
    TT(P, 'dve', X, X[:, :], N0, N0[:, :], cst, cst[:, CC_ID:CC_ID + 128], ALU.add)
    Ncur, NTcur = N0, NT0
    for lev in range(1, 7):
        last = (lev == 6)
        pnt = k.psum.get()
        MM(P, pnt, pnt[:, 0:128], Ncur, Ncur[:, :], NTcur, NTcur[:, :])
        if not last:
            pn = k.psum.get()
            MM(P, pn, pn[:, 0:128], NTcur, NTcur[:, :], Ncur, Ncur[:, :])
        NTn = k.n_pool.get()
        P.op('act', lambda e, NTn=NTn, pnt=pnt: e.copy(NTn[:, :], pnt[:, 0:128]), reads=[pnt], writes=[NTn])
        if not last:
            Nn = k.n_pool.get()
            P.op('act', lambda e, Nn=Nn, pn=pn: e.copy(Nn[:, :], pn[:, 0:128]), reads=[pn], writes=[Nn])
        px = k.psum.get()
        MM(P, px, px[:, 0:128], NTn, NTn[:, :], X, X[:, :])
        Xn = k.x_pool.get()
        TT(P, 'dve', Xn, Xn[:, :], px, px[:, 0:128], X, X[:, :], ALU.add)
        X = Xn
        NTcur = NTn
        if not last:
            Ncur = Nn
    return X


def rstd_from_ps(P, k, ps, n, eps_col, scale, parts=128):
    r = k.r_pool.get()
    ACTF(P, r, r[0:parts, 0:n], ps, ps[0:parts, 0:n], AF.Sqrt, bias=k.epsc[0:parts, eps_col:eps_col + 1], scale=scale, rd=[k.epsc])
    P.op('dve', lambda e: e.reciprocal(r[0:parts, 0:n], r[0:parts, 0:n]), reads=[r], writes=[r])
    return r


def pass_a(P, k, uT, wA, mA_d):
    cst, par = k.cst, k.par
    import os
    STOPA = int(os.environ.get('STOPA', 0))
    for c in range(int(os.environ.get('NCHA', NCH))):
        t0 = c * C
        qkv = []
        for i in range(12):
            ps = k.psum.get()
            for dc in range(8):
                MM(P, ps, ps[:, 0:C + HALO], wA, wA[:, dc, i * 128:(i + 1) * 128], uT, uT[:, dc, t0:t0 + C + HALO], dc == 0, dc == 7)
            acc = k.t_pool.get()
            w0 = PC_CONV + i * 4
            TS(P, 'dve', acc, acc[:, :], ps, ps[:, 0:C], par[:, w0:w0 + 1], None, ALU.mult, rd=[par])
            for j in range(1, 4):
                STT(P, acc, acc[:, :], ps, ps[:, j:j + C], par[:, w0 + j:w0 + j + 1], acc, acc[:, :], ALU.mult, ALU.add, rd=[par])
            t = k.qkv_tiles[i]
            ACTF(P, t, t[:, :], acc, acc[:, :], AF.Silu)
            qkv.append(t)
        if STOPA == 1:
            return
        for i in range(8):
            sq = k.t_pool.get()
            ACTF(P, sq, sq[:, :], qkv[i], qkv[i][:, :], AF.Square)
            ps = k.psum.get()
            MM(P, ps, ps[:, 0:C], cst, cst[:, CC_ONE:CC_ONE + 128], sq, sq[:, :])
            r = rstd_from_ps(P, k, ps, C, 0, 1.0)
            STT(P, qkv[i], qkv[i][:, :], qkv[i], qkv[i][:, :], (128 ** -0.5) if i < 4 else 1.0, r, r[:, 0:C], ALU.mult, ALU.mult)
        if STOPA == 2:
            return
        G = []
        for i in range(4):
            ps = k.psum.get()
            for dc in range(8):
                MM(P, ps, ps[:, 0:C], wA, wA[:, dc, 1536 + i * 128:1536 + (i + 1) * 128], uT, uT[:, dc, t0 + HALO:t0 + HALO + C], dc == 0, dc == 7)
            zs = k.t_pool.get()
            ACTF(P, zs, zs[:, :], ps, ps[:, 0:C], AF.Silu)
            ps2 = k.psum.get()
            for dc in range(8):
                MM(P, ps2, ps2[:, 0:C], wA, wA[:, dc, 2048 + i * 128:2048 + (i + 1) * 128], uT, uT[:, dc, t0 + HALO:t0 + HALO + C], dc == 0, dc == 7)
            th = k.t_pool.get()
            ACTF(P, th, th[:, :], ps2, ps2[:, 0:C], AF.Tanh, scale=0.5)
            g = k.g_tiles[i]
            STT(P, g, g[:, :], th, th[:, :], 1.0, zs, zs[:, :], ALU.add, ALU.mult)
            G.append(g)
        if STOPA == 3:
            return
        psb = k.psum.get()
        for dc in range(8):
            MM(P, psb, psb[0:4, 0:C], wA, wA[:, dc, 2560:2564], uT, uT[:, dc, t0 + HALO:t0 + HALO + C], dc == 0, dc == 7)
        psa = k.psum.get()
        for dc in range(8):
            MM(P, psa, psa[0:4, 0:C], wA, wA[:, dc, 2564:2568], uT, uT[:, dc, t0 + HALO:t0 + HALO + C], dc == 0, dc == 7)
        rows = k.rows_pool.get()
        ACTF(P, rows, rows[0:4, 0, :], psb, psb[0:4, 0:C], AF.Tanh, scale=0.5)
        TS(P, 'dve', rows, rows[0:4, 0, :], rows, rows[0:4, 0, :], 0.5, 0.5, ALU.mult, ALU.add)
        ACTF(P, rows, rows[0:4, 3, :], psa, psa[0:4, 0:C], AF.Exp, bias=k.hp[0:4, 1:2], rd=[k.hp])
        ACTF(P, rows, rows[0:4, 3, :], rows, rows[0:4, 3, :], AF.Ln, bias=k.epsc[0:4, 3:4], rd=[k.epsc])
        TS(P, 'dve', rows, rows[0:4, 1, :], rows, rows[0:4, 3, :], k.hp[0:4, 2:3], None, ALU.mult, rd=[k.hp])
        P.op('dve', lambda e, rows=rows: e.tensor_tensor_scan(rows[0:4, 2, :], k.onesrow[0:4, 0:C], rows[0:4, 1, :], 0.0, ALU.mult, ALU.add),
             reads=[rows, k.onesrow], writes=[rows])
        if STOPA == 4:
            return
        pc = k.psum.get()
        MM(P, pc, pc[:, 0:4], rows, rows[0:4, 2, :], k.sel, k.sel[0:4, 512:516])
        MM(P, pc, pc[:, 4:8], rows, rows[0:4, 0, :], k.sel, k.sel[0:4, 512:516])
        dg = k.dg_pool.get()
        TS(P, 'dve', dg, dg[0:4, 0:4], k.sel, k.sel[0:4, 512:516], rows[0:4, 2, C - 1:C], None, ALU.mult, rd=[rows])
        MM(P, pc, pc[:, 8:12], k.sel, k.sel[0:4, 516:644], dg, dg[0:4, 0:4])
        cols = k.cols_pool.get()
        P.op('act', lambda e, cols=cols, pc=pc: e.copy(cols[:, 0:12], pc[:, 0:12]), reads=[pc], writes=[cols])
        ACTF(P, cols, cols[:, 12:16], cols, cols[:, 0:4], AF.Exp)
        TT(P, 'dve', cols, cols[:, 12:16], cols, cols[:, 12:16], cols, cols[:, 4:8], ALU.mult)
        TT(P, 'dve', cols, cols[:, 16:20], cols, cols[:, 8:12], cols, cols[:, 0:4], ALU.subtract)
        ACTF(P, cols, cols[:, 16:20], cols, cols[:, 16:20], AF.Exp)
        ACTF(P, cols, cols[:, 20:24], cols, cols[:, 8:12], AF.Exp)
        if STOPA == 5:
            return
        pgr = k.psum.get()
        pbr = k.psum.get()
        for h in range(4):
            MM(P, pgr, pgr[:, h * 128:(h + 1) * 128], k.sel, k.sel[0:4, h * 128:(h + 1) * 128], rows, rows[0:4, 2, :])
            MM(P, pbr, pbr[:, h * 128:(h + 1) * 128], k.sel, k.sel[0:4, h * 128:(h + 1) * 128], rows, rows[0:4, 0, :])
        grow = k.bc_pool.get()
        brow = k.bc_pool.get()
        egrow = k.bc_pool.get()
        P.op('act', lambda e, grow=grow, pgr=pgr: e.copy(grow[:, :], pgr[:, :]), reads=[pgr], writes=[grow])
        P.op('act', lambda e, brow=brow, pbr=pbr: e.copy(brow[:, :], pbr[:, :]), reads=[pbr], writes=[brow])
        ACTF(P, egrow, egrow[:, :], pgr, pgr[:, :], AF.Exp)
        if STOPA == 6:
            return
        for h in range(4):
            hs = slice(h * 128, (h + 1) * 128)
            qn, kn, vs = qkv[h], qkv[4 + h], qkv[8 + h]
            kbT = k.t_pool.get()
            TT(P, 'dve', kbT, kbT[:, :], kn, kn[:, :], brow, brow[:, hs], ALU.mult)
            d1 = k.t_pool.get()
            TS(P, 'dve', d1, d1[:, :], grow, grow[:, hs], cols[:, h:h + 1], 0.0, ALU.subtract, ALU.max, rd=[cols])
            ACTF(P, d1, d1[:, :], d1, d1[:, :], AF.Exp, scale=-1.0)
            d2 = k.t_pool.get()
            TS(P, 'dve', d2, d2[:, :], grow, grow[:, hs], cols[:, h:h + 1], 0.0, ALU.subtract, ALU.min, rd=[cols])
            ACTF(P, d2, d2[:, :], d2, d2[:, :], AF.Exp)
            gm = k.t_pool.get()
            TT(P, 'pool', gm, gm[:, :], d1, d1[:, :], cst, cst[:, CC_SL:CC_SL + 128], ALU.mult)
            gmt = k.t_pool.get()
            TT(P, 'pool', gmt, gmt[:, :], d2, d2[:, :], cst, cst[:, CC_SU:CC_SU + 128], ALU.mult)
            git = k.t_pool.get()
            TT(P, 'pool', git, git[:, :], d2, d2[:, :], cst, cst[:, CC_IU:CC_IU + 128], ALU.mult)
            pA = k.psum.get()
            MM(P, pA, pA[:, 0:128], kbT, kbT[:, :], kn, kn[:, :])
            pAT = k.psum.get()
            MM(P, pAT, pAT[:, 0:128], kn, kn[:, :], kbT, kbT[:, :])
            NT0 = k.n_pool.get()
            STT(P, NT0, NT0[:, :], pA, pA[:, 0:128], -1.0, gm, gm[:, :], ALU.mult, ALU.mult)
            N0 = k.n_pool.get()
            STT(P, N0, N0[:, :], pAT, pAT[:, 0:128], -1.0, gmt, gmt[:, :], ALU.mult, ALU.mult)
            if STOPA == 7:
                return
            X = tri_inverse(P, k, NT0, N0)
            pat = k.psum.get()
            MM(P, pat, pat[:, 0:128], kn, kn[:, :], qn, qn[:, :])
            attnT = k.t_pool.get()
            TT(P, 'dve', attnT, attnT[:, :], pat, pat[:, 0:128], git, git[:, :], ALU.mult)
            if STOPA == 8:
                return
            pk = k.psum.get()
            MM(P, pk, pk[:, 0:128], kn, kn[:, :], cst, cst[:, CC_ID:CC_ID + 128])
            kbg = k.t_pool.get()
            TS(P, 'dve', kbg, kbg[:, :], pk, pk[:, 0:128], cols[:, 12 + h:13 + h], None, ALU.mult, rd=[cols])
            kt = k.t_pool.get()
            TS(P, 'dve', kt, kt[:, :], pk, pk[:, 0:128], cols[:, 16 + h:17 + h], None, ALU.mult, rd=[cols])
            pv = k.psum.get()
            MM(P, pv, pv[:, 0:128], vs, vs[:, :], cst, cst[:, CC_ID:CC_ID + 128])
            vb = k.t_pool.get()
            TS(P, 'dve', vb, vb[:, :], pv, pv[:, 0:128], cols[:, 4 + h:5 + h], None, ALU.mult, rd=[cols])
            qgT = k.t_pool.get()
            TT(P, 'pool', qgT, qgT[:, :], qn, qn[:, :], egrow, egrow[:, hs], ALU.mult)
            pu = k.psum.get()
            MM(P, pu, pu[:, 0:128], X, X[:, :], vb, vb[:, :])
            u = k.t_pool.get()
            P.op('act', lambda e, u=u, pu=pu: e.copy(u[:, :], pu[:, 0:128]), reads=[pu], writes=[u])
            pw = k.psum.get()
            MM(P, pw, pw[:, 0:128], kbg, kbg[:, :], X, X[:, :])
            wkT = k.t_pool.get()
            P.op('act', lambda e, wkT=wkT, pw=pw: e.copy(wkT[:, :], pw[:, 0:128]), reads=[pw], writes=[wkT])
            if STOPA == 9:
                return
            S = k.SA[h]
            p1 = k.psum.get()
            MM(P, p1, p1[:, 0:128], wkT, wkT[:, :], S, S[:, :])
            vnew = k.t_pool.get()
            TT(P, 'dve', vnew, vnew[:, :], u, u[:, :], p1, p1[:, 0:128], ALU.subtract)
            po = k.psum.get()
            MM(P, po, po[:, 0:128], S, S[:, :], qgT, qgT[:, :], True, False)
            MM(P, po, po[:, 0:128], vnew, vnew[:, :], attnT, attnT[:, :], False, True)
            pS = k.psum.get()
            MM(P, pS, pS[:, 0:128], kt, kt[:, :], vnew, vnew[:, :])
            STT(P, S, S[:, :], S, S[:, :], cols[:, 20 + h:21 + h], pS, pS[:, 0:128], ALU.mult, ALU.add, rd=[cols])
            if STOPA == 10:
                return
            osq = k.t_pool.get()
            ACTF(P, osq, osq[:, :], po, po[:, 0:128], AF.Square)
            pn = k.psum.get()
            MM(P, pn, pn[:, 0:128], cst, cst[:, CC_ONE:CC_ONE + 128], osq, osq[:, :])
            r = rstd_from_ps(P, k, pn, C, 1, 1.0 / 128)
            om = k.t_pool.get()
            TT(P, 'dve', om, om[:, :], po, po[:, 0:128], r, r[:, 0:C], ALU.mult)
            mo = k.mo_pool.get()
            STT(P, mo, mo[:, :], om, om[:, :], par[:, PC_OGH:PC_OGH + 1], G[h], G[h][:, :], ALU.mult, ALU.mult, rd=[par])
            dt_ = Tile(None, 'mAd')
            k.mA_tiles[(c, h)] = dt_
            P.op('sp', lambda e, mo=mo, h=h, t0=t0: e.dma_start(out=mA_d[h * 128:(h + 1) * 128, t0:t0 + C], in_=mo[:, :]),
                 reads=[mo], writes=[dt_], dma=True)


def pass_b(P, k, uT, wB, mA_d, mrg_d, dbg_out=None):
    cst, par = k.cst, k.par
    H1 = 1
    import os
    for c in range(int(os.environ.get('NCHB', NCH))):
        t0 = c * C
        o0 = t0 + HALO - H1
        rkv = []
        for i in range(12):
            ps = k.psum.get()
            for dc in range(8):
                MM(P, ps, ps[:, 0:C + 1], wB, wB[:, dc, i * 128:(i + 1) * 128], uT, uT[:, dc, o0:o0 + C + 1], dc == 0, dc == 7)
            tm = k.t_pool.get()
            TS(P, 'dve', tm, tm[:, :], ps, ps[:, 0:C], par[:, PC_MU + i:PC_MU + i + 1], None, ALU.mult, rd=[par])
            t = k.qkv_tiles[i]
            STT(P, t, t[:, :], ps, ps[:, 1:C + 1], par[:, PC_OMM + i:PC_OMM + i + 1], tm, tm[:, :], ALU.mult, ALU.add, rd=[par])
            rkv.append(t)
        GB = []
        for i in range(4):
            ps = k.psum.get()
            for dc in range(8):
                MM(P, ps, ps[:, 0:C], wB, wB[:, dc, 1536 + i * 128:1536 + (i + 1) * 128], uT, uT[:, dc, t0 + HALO:t0 + HALO + C], dc == 0, dc == 7)
            g = k.g_tiles[i]
            ACTF(P, g, g[:, :], ps, ps[:, 0:C], AF.Tanh, scale=0.5)
            GB.append(g)
        lor = []
        for j, (c0, n) in enumerate([(2048, 64), (2112, 64), (2176, 128), (2304, 32)]):
            ps = k.psum.get()
            for dc in range(8):
                MM(P, ps, ps[0:n, 0:C + 1], wB, wB[:, dc, c0:c0 + n], uT, uT[:, dc, o0:o0 + C + 1], dc == 0, dc == 7)
            tm = k.t_pool.get()
            TS(P, 'dve', tm, tm[0:n, :], ps, ps[0:n, 0:C], par[0:n, PC_MUL + j:PC_MUL + j + 1], None, ALU.mult, rd=[par])
            t = k.lor_tiles[j]
            STT(P, t, t[0:n, :], ps, ps[0:n, 1:C + 1], par[0:n, PC_OMML + j:PC_OMML + j + 1], tm, tm[0:n, :], ALU.mult, ALU.add, rd=[par])
            lor.append(t)
        ACTF(P, lor[0], lor[0][0:64, :], lor[0], lor[0][0:64, :], AF.Tanh)
        for j, n in ((2, 128), (3, 32)):
            ACTF(P, lor[j], lor[j][0:n, :], lor[j], lor[j][0:n, :], AF.Tanh, scale=0.5)
            TS(P, 'dve', lor[j], lor[j][0:n, :], lor[j], lor[j][0:n, :], 1.0, 0.5, ALU.add, ALU.mult)
        Ypair = []
        mBs = []
        for i in range(4):
            cs = slice(i * 128, (i + 1) * 128)
            rf, kf, vf = rkv[i], rkv[4 + i], rkv[8 + i]
            ps = k.psum.get()
            MM(P, ps, ps[:, 0:C], k.lw, k.lw[0:64, 0, cs], lor[0], lor[0][0:64, :])
            lw = k.t_pool.get()
            ACTF(P, lw, lw[:, :], ps, ps[:, 0:C], AF.Tanh, bias=par[:, PC_W0H + i:PC_W0H + i + 1], scale=0.5, rd=[par])
            TS(P, 'dve', lw, lw[:, :], lw, lw[:, :], 1.0, -0.5 * float(np.exp(-0.5)), ALU.add, ALU.mult)
            ps = k.psum.get()
            MM(P, ps, ps[:, 0:C], k.lw, k.lw[0:64, 1, cs], lor[1], lor[1][0:64, :])
            asig = k.t_pool.get()
            ACTF(P, asig, asig[:, :], ps, ps[:, 0:C], AF.Tanh, bias=par[:, PC_A0H + i:PC_A0H + i + 1], scale=0.5, rd=[par])
            TS(P, 'dve', asig, asig[:, :], asig, asig[:, :], 1.0, 0.5, ALU.add, ALU.mult)
            psg = k.psum.get()
            MM(P, psg, psg[:, 0:C], k.lg, k.lg[:, 0, cs], lor[2], lor[2][:, :], True, False)
            MM(P, psg, psg[:, 0:C], k.lg, k.lg[0:32, 1, cs], lor[3], lor[3][0:32, :], False, True)
            gate = k.t_pool.get()
            STT(P, gate, gate[:, :], GB[i], GB[i][:, :], 1.0, psg, psg[:, 0:C], ALU.add, ALU.mult)
            kk = k.t_pool.get()
            TS(P, 'dve', kk, kk[:, :], kf, kf[:, :], par[:, PC_KK + i:PC_KK + i + 1], None, ALU.mult, rd=[par])
            sq = k.t_pool.get()
            ACTF(P, sq, sq[:, :], kk, kk[:, :], AF.Square)
            ps = k.psum.get()
            MM(P, ps, ps[:, 0:C], cst, cst[:, CC_BLK:CC_BLK + 128], sq, sq[:, :])
            r = rstd_from_ps(P, k, ps, C, 0, 1.0)
            TT(P, 'dve', kk, kk[:, :], kk, kk[:, :], r, r[:, 0:C], ALU.mult)
            km = k.t_pool.get()
            TS(P, 'dve', km, km[:, :], asig, asig[:, :], -1.0, None, ALU.add)
            TS(P, 'dve', km, km[:, :], km, km[:, :], par[:, PC_KA + i:PC_KA + i + 1], None, ALU.mult, rd=[par])
            STT(P, km, km[:, :], km, km[:, :], 1.0, kf, kf[:, :], ALU.add, ALU.mult)
            pr = k.t_pool.get()
            STT(P, pr, pr[:, :], rf, rf[:, :], par[:, PC_RK + i:PC_RK + i + 1], km, km[:, :], ALU.mult, ALU.mult, rd=[par])
            prk = k.psum.get()
            MM(P, prk, prk[:, 0:C], cst, cst[:, CC_BLK:CC_BLK + 128], pr, pr[:, :])
            rkv_t = k.t_pool.get()
            TT(P, 'dve', rkv_t, rkv_t[:, :], prk, prk[:, 0:C], vf, vf[:, :], ALU.mult)
            cumI = k.t_pool.get()
            P.op('dve', lambda e, cumI=cumI, lw=lw: e.tensor_tensor_scan(cumI[:, :], k.onesrow[:, 0:C], lw[:, :], 0.0, ALU.mult, ALU.add),
                 reads=[lw, k.onesrow], writes=[cumI])
            eE = k.t_pool.get()
            TT(P, 'dve', eE, eE[:, :], cumI, cumI[:, :], lw, lw[:, :], ALU.subtract)
            ACTF(P, eE, eE[:, :], eE, eE[:, :], AF.Exp)
            eI = k.t_pool.get()
            ACTF(P, eI, eI[:, :], cumI, cumI[:, :], AF.Exp)
            eN = k.t_pool.get()
            ACTF(P, eN, eN[:, :], cumI, cumI[:, :], AF.Exp, scale=-1.0)
            atil = k.t_pool.get()
            STT(P, atil, atil[:, :], kk, kk[:, :], -1.0, eE, eE[:, :], ALU.mult, ALU.mult)
            rtil = k.t_pool.get()
            TT(P, 'pool', rtil, rtil[:, :], rf, rf[:, :], eI, eI[:, :], ALU.mult)
            btil = k.t_pool.get()
            TT(P, 'dve', btil, btil[:, :], kk, kk[:, :], asig, asig[:, :], ALU.mult)
            TT(P, 'dve', btil, btil[:, :], btil, btil[:, :], eN, eN[:, :], ALU.mult)
            ktil = k.t_pool.get()
            TT(P, 'pool', ktil, ktil[:, :], km, km[:, :], eN, eN[:, :], ALU.mult)
            bh = k.t_pool.get()
            TS(P, 'dve', bh, bh[:, :], btil, btil[:, :], eI[:, C - 1:C], None, ALU.mult, rd=[eI])
            kh = k.t_pool.get()
            TS(P, 'dve', kh, kh[:, :], ktil, ktil[:, :], eI[:, C - 1:C], None, ALU.mult, rd=[eI])
            wc = k.wc_pool.get()
            P.op('act', lambda e, wc=wc, eI=eI: e.copy(wc[:, 0:1], eI[:, C - 1:C]), reads=[eI], writes=[wc])
            tms = []
            for src in (bh, kh, vf):
                pt = k.psum.get()
                MM(P, pt, pt[:, 0:128], src, src[:, :], cst, cst[:, CC_ID:CC_ID + 128])
                tmt = k.t_pool.get()
                P.op('act', lambda e, tmt=tmt, pt=pt: e.copy(tmt[:, :], pt[:, 0:128]), reads=[pt], writes=[tmt])
                tms.append(tmt)
            Bh, Kh, Vt = tms
            ST = k.SB[i]
            Pp = k.t_pool.get()
            Ysb = k.t_pool.get()
            for j in range(2):
                rs = slice(64 * j, 64 * j + 64)
                pab = k.psum.get()
                MM(P, pab, pab[:, 0:128], atil, atil[rs, :], btil, btil[rs, :])
                pabT = k.psum.get()
                MM(P, pabT, pabT[:, 0:128], btil, btil[rs, :], atil, atil[rs, :])
                NT0 = k.n_pool.get()
                TT(P, 'dve', NT0, NT0[:, :], pab, pab[:, 0:128], cst, cst[:, CC_SL:CC_SL + 128], ALU.mult)
                N0 = k.n_pool.get()
                TT(P, 'dve', N0, N0[:, :], pabT, pabT[:, 0:128], cst, cst[:, CC_SU:CC_SU + 128], ALU.mult)
                X = tri_inverse(P, k, NT0, N0)
                mats = []
                for (la, ra, mk) in ((ktil, atil, CC_SU), (btil, rtil, CC_IU), (ktil, rtil, CC_IU)):
                    pm = k.psum.get()
                    MM(P, pm, pm[:, 0:128], la, la[rs, :], ra, ra[rs, :])
                    mt = k.t_pool.get()
                    TT(P, 'dve', mt, mt[:, :], pm, pm[:, 0:128], cst, cst[:, mk:mk + 128], ALU.mult)
                    mats.append(mt)
                AakT, ArbT, ArkT = mats
                pG = k.psum.get()
                MM(P, pG, pG[:, 0:64], atil, atil[rs, :], ST, ST[rs, :], True, False)
                MM(P, pG, pG[:, 0:64], AakT, AakT[:, :], Vt, Vt[:, rs], False, True)
                Gs = k.t_pool.get()
                P.op('act', lambda e, Gs=Gs, pG=pG: e.copy(Gs[:, 0:64], pG[:, 0:64]), reads=[pG], writes=[Gs])
                pP = k.psum.get()
                MM(P, pP, pP[:, 0:64], X, X[:, :], Gs, Gs[:, 0:64])
                P.op('act', lambda e, Pp=Pp, pP=pP, rs=rs: e.copy(Pp[:, rs], pP[:, 0:64]), reads=[pP], writes=[Pp])
                pY = k.psum.get()
                MM(P, pY, pY[:, 0:64], rtil, rtil[rs, :], ST, ST[rs, :], True, False)
                MM(P, pY, pY[:, 0:64], ArbT, ArbT[:, :], Pp, Pp[:, rs], False, False)
                MM(P, pY, pY[:, 0:64], ArkT, ArkT[:, :], Vt, Vt[:, rs], False, True)
                P.op('act', lambda e, Ysb=Ysb, pY=pY, rs=rs: e.copy(Ysb[:, rs], pY[:, 0:64]), reads=[pY], writes=[Ysb])
            pS = k.psum.get()
            MM(P, pS, pS[:, 0:128], Bh, Bh[:, :], Pp, Pp[:, :], True, False)
            MM(P, pS, pS[:, 0:128], Kh, Kh[:, :], Vt, Vt[:, :], False, True)
            for j in range(2):
                rs = slice(64 * j, 64 * j + 64)
                STT(P, ST, ST[rs, :], ST, ST[rs, :], wc[rs, 0:1], pS, pS[rs, 64 * j:64 * j + 64], ALU.mult, ALU.add, rd=[wc])
            pyt = k.psum.get()
            MM(P, pyt, pyt[:, 0:128], Ysb, Ysb[:, :], cst, cst[:, CC_ID:CC_ID + 128])
            yT = k.t_pool.get()
            P.op('act', lambda e, yT=yT, pyt=pyt: e.copy(yT[:, :], pyt[:, 0:128]), reads=[pyt], writes=[yT])
            pmn = k.psum.get()
            MM(P, pmn, pmn[:, 0:C], cst, cst[:, CC_BLK:CC_BLK + 128], yT, yT[:, :])
            yc = k.t_pool.get()
            STT(P, yc, yc[:, :], pmn, pmn[:, 0:C], -1.0 / 64, yT, yT[:, :], ALU.mult, ALU.add)
            ysq = k.t_pool.get()
            ACTF(P, ysq, ysq[:, :], yc, yc[:, :], AF.Square)
            pvr = k.psum.get()
            MM(P, pvr, pvr[:, 0:C], cst, cst[:, CC_BLK:CC_BLK + 128], ysq, ysq[:, :])
            r = rstd_from_ps(P, k, pvr, C, 2, 1.0 / 64)
            TT(P, 'dve', yc, yc[:, :], yc, yc[:, :], r, r[:, 0:C], ALU.mult)
            TS(P, 'dve', yc, yc[:, :], yc, yc[:, :], par[:, PC_LNG + i:PC_LNG + i + 1], par[:, PC_LNB + i:PC_LNB + i + 1], ALU.mult, ALU.add, rd=[par])
            TT(P, 'dve', yc, yc[:, :], yc, yc[:, :], rkv_t, rkv_t[:, :], ALU.add)
            mb = k.t_pool.get()
            STT(P, mb, mb[:, :], yc, yc[:, :], 0.5, gate, gate[:, :], ALU.mult, ALU.mult)
            if dbg_out is not None:
                import os
                nm = os.environ.get('DBGB', 'mb')
                src = dict(mb=mb, rf=rf, kf=kf, vf=vf, lw=lw, asig=asig, gate=gate, kk=kk, km=km, yT=yT, yc=yc, rkv_t=rkv_t, atil=atil, rtil=rtil, btil=btil, ktil=ktil, cumI=cumI)[nm]
                P.op('sp', lambda e, src=src, i=i, t0=t0: e.dma_start(out=dbg_out[i * 128:(i + 1) * 128, t0:t0 + C], in_=src[:, :]),
                     reads=[src], writes=[k.new_mrg(i, 0)], dma=True)
                continue
            ma = k.ma_pool.get()
            P.op('sp', lambda e, ma=ma, i=i, t0=t0: e.dma_start(out=ma[:, :], in_=mA_d[i * 128:(i + 1) * 128, t0:t0 + C]),
                 reads=[k.mA_tiles[(c, i)]] if (c, i) in k.mA_tiles else [], writes=[ma], dma=True)
            mo = k.mob_pool.get()
            TT(P, 'dve', mo, mo[:, :], mb, mb[:, :], ma, ma[:, :], ALU.add)
            segs = []
            if t0 + C <= TH:
                segs.append((0, t0, 0, C))
            elif t0 >= TH:
                segs.append((1, t0 - TH, 0, C))
            else:
                n0 = TH - t0
                segs.append((0, t0, 0, n0))
                segs.append((1, 0, n0, C - n0))
            for (hf, d0, s0, n) in segs:
                P.op('sp', lambda e, mo=mo, i=i, hf=hf, d0=d0, s0=s0, n=n: e.dma_start(out=mrg_d[i][hf][:, d0:d0 + n], in_=mo[:, s0:s0 + n]),
                     reads=[mo], writes=[k.new_mrg(i, hf)], dma=True)


def build(stage=9):
    nc = bass.Bass("TRN2", target_bir_lowering=False)
    k = K()
    dt = lambda name, shape, dty=F32, kind="ExternalInput": nc.dram_tensor(name, shape, dty, kind=kind).ap()
    xT_d = dt("xT", [D, TH])
    par_d = dt("par", [128, NPAR])
    cst_d = dt("cst", [128, NCONST])
    sel_d = dt("sel", [4, 644])
    hp_d = dt("hp", [4, 4])
    wgu1_d = dt("wgu1", [NFC, 128, 8, 256])
    wd1_d = dt("wd1", [NFC, 128, D])
    wgu2_d = dt("wgu2", [NFC, 128, 8, 256])
    wd2_d = dt("wd2", [NFC, 128, D])
    wA_d = dt("wA", [128, 8, NA])
    wB_d = dt("wB", [128, 8, NB])
    lw_d = dt("lw", [64, 2, 512])
    lg_d = dt("lg", [128, 2, 512])
    wo_d = dt("wo", [128, 8, D])
    out_d = dt("outT", [D, TH], F32, "ExternalOutput")
    u_src = [nc.dram_tensor(f"u_src{i}", [128, TH], BF16).ap() for i in range(8)]
    u_all = [nc.dram_tensor(f"u_all{i}", [256, TH], BF16).ap() for i in range(8)]
    h_sp = nc.dram_tensor("h_sp", [D, TH], F32).ap()
    mA_d = nc.dram_tensor("mA", [512, TP], F32).ap()
    mrg_d = [[nc.dram_tensor(f"mrg{i}_{hf}", [128, TH], BF16).ap() for hf in range(2)] for i in range(4)]
    mrg_all = [[nc.dram_tensor(f"mrga{i}_{hf}", [256, TH], BF16).ap() for hf in range(2)] for i in range(4)]
    rg = [[0, 1], [2, 3], [4, 5], [6, 7]]

    P = Prog(nc)
    ar = Arena(nc, "arena", 206 * 1024)
    k.psum = Pool([Tile(nc.alloc_psum_tensor(f"ps{i}", [128, 512], F32).ap(), f"ps{i}") for i in range(8)])
    k.par = ar.alloc([NPAR], F32, 'par')
    k.cst = ar.alloc([NCONST], F32, 'cst')
    k.ones_bf = ar.alloc([128], BF16, 'ones_bf')
    k.epsc = ar.alloc([4], F32, 'epsc')
    k.sel = ar.alloc([644], F32, 'sel')
    k.hp = ar.alloc([4], F32, 'hp')
    k.onesrow = ar.alloc([128], F32, 'onesrow')
    base0 = ar.off
    hT = ar.alloc([8, TH], F32, 'hT')
    base1 = ar.off

    def ffn_alloc():
        ar.reset(base1)
        k.hn = ar.alloc([8, STW], BF16, 'hn')
        k.act_tiles = [ar.alloc([STW], BF16, f'act{i}') for i in range(NFC)]
        k.wgu_pool = Pool([ar.alloc([8, 256], BF16, f'wgu{i}') for i in range(2)])
        k.wd_res = [None] * NFC
        k.wd_pool = Pool([ar.alloc([D], BF16, f'wd{i}') for i in range(NFC)])
        k.sq_pool = Pool([ar.alloc([8, NT], BF16, f'sq{i}') for i in range(1)])
        k.rstd_pool = Pool([ar.alloc([NT], F32, f'rstd{i}') for i in range(2)])
        k.sg_pool = Pool([ar.alloc([NT], F32, f'sg{i}') for i in range(2)])
        k.ntmp_pool = Pool([ar.alloc([NT], F32, f'ntmp{i}') for i in range(2)])

    P.op('sp', lambda e: e.dma_start(out=hT[:, :, :], in_=xT_d.rearrange("(c p) t -> p c t", p=128)), writes=[hT], dma=True)
    P.op('sp', lambda e: e.dma_start(out=k.par[:, :], in_=par_d), writes=[k.par], dma=True)
    P.op('sp', lambda e: e.dma_start(out=k.cst[:, :], in_=cst_d), writes=[k.cst], dma=True)
    P.op('sp', lambda e: e.dma_start(out=k.sel[0:4, :], in_=sel_d), writes=[k.sel], dma=True)
    P.op('sp', lambda e: e.dma_start(out=k.hp[0:4, :], in_=hp_d), writes=[k.hp], dma=True)
    P.op('dve', lambda e: e.memset(k.ones_bf[:, :], 1.0 / D), writes=[k.ones_bf])
    P.op('dve', lambda e: e.memset(k.onesrow[:, :], 1.0), writes=[k.onesrow])
    P.op('dve', lambda e: e.memset(k.epsc[:, 0:1], 1e-6), writes=[k.epsc])
    P.op('dve', lambda e: e.memset(k.epsc[:, 1:2], EPS), writes=[k.epsc])
    P.op('dve', lambda e: e.memset(k.epsc[:, 2:3], GN_EPS), writes=[k.epsc])
    P.op('dve', lambda e: e.memset(k.epsc[:, 3:4], 1.0), writes=[k.epsc])
    k.cst_eps = k.epsc
    TS(P, 'dve', k.par, k.par[:, PC_W0H:PC_W0H + 8], k.par, k.par[:, PC_W0:PC_W0 + 8], 0.5, None, ALU.mult)
    TS(P, 'dve', k.par, k.par[:, PC_OGH:PC_OGH + 1], k.par, k.par[:, PC_OG:PC_OG + 1], 0.5, None, ALU.mult)
    TS(P, 'dve', k.par, k.par[:, PC_OMM:PC_OMM + 12], k.par, k.par[:, PC_MU:PC_MU + 12], -1.0, 1.0, ALU.mult, ALU.add)
    TS(P, 'dve', k.par, k.par[:, PC_OMML:PC_OMML + 4], k.par, k.par[:, PC_MUL:PC_MUL + 4], -1.0, 1.0, ALU.mult, ALU.add)
    ACTF(P, k.hp, k.hp[0:4, 2:3], k.hp, k.hp[0:4, 0:1], AF.Exp)
    TS(P, 'dve', k.hp, k.hp[0:4, 2:3], k.hp, k.hp[0:4, 2:3], -1.0, None, ALU.mult)

    ffn_alloc()
    ffn(P, k, hT, wgu1_d, wd1_d, PC_FFN1)
    P.barrier()
    ar.reset(base1)
    k.sq_pool = Pool([ar.alloc([8, NT], BF16, f'sqb{i}') for i in range(2)])
    k.rstd_pool = Pool([ar.alloc([NT], F32, f'rstdb{i}') for i in range(2)])
    k.ntmp_pool = Pool([ar.alloc([NT], F32, f'ntmpb{i}') for i in range(2)])
    un = ar.alloc([8, TH], BF16, 'un')
    for j in range(TH // NT):
        rmsnorm_tile(P, k, hT, PC_MIX, un, slice(j * NT, (j + 1) * NT), slice(j * NT, (j + 1) * NT), NT)
    usrc_t = [Tile(None, f'usrc{i}') for i in range(8)]
    for dc in range(8):
        P.op('sp', lambda e, dc=dc: e.dma_start(out=u_src[dc], in_=un[:, dc, :]), reads=[un], writes=[usrc_t[dc]], dma=True)
    hsp_t = Tile(None, 'hsp')
    P.op('sp', lambda e: e.dma_start(out=h_sp.rearrange("(c p) t -> p c t", p=128), in_=hT[:, :, :]), reads=[hT], writes=[hsp_t], dma=True)
    uall_t = [Tile(None, f'uall{i}') for i in range(8)]
    for dc in range(8):
        P.op('pool', lambda e, dc=dc: e.collective_compute("AllGather", ALU.bypass, replica_groups=rg, ins=[u_src[dc].opt()], outs=[u_all[dc].opt()]),
             reads=[usrc_t[dc]], writes=[uall_t[dc]], cc=True)
    P.barrier()
    ar.reset(base0)
    uT = ar.alloc([8, HALO + TP], BF16, 'uT')
    P.op('dve', lambda e: e.memset(uT[:, :, 0:HALO], 0.0), writes=[uT])
    for r_ in range(2):
        for dc in range(8):
            P.op('sp', lambda e, r_=r_, dc=dc: e.dma_start(out=uT[:, dc, HALO + r_ * TH:HALO + (r_ + 1) * TH],
                                                          in_=u_all[dc][r_ * 128:(r_ + 1) * 128, :]),
                 reads=[uall_t[dc]], writes=[uT], dma=True)
    base2 = ar.off
    wA = ar.alloc([8, NA], BF16, 'wA')
    for dc in range(8):
        P.op('pool', lambda e, dc=dc: e.dma_start(out=wA[:, dc, :], in_=wA_d[:, dc, :]), writes=[wA], dma=True)
    f32t = lambda name: ar.alloc([128], F32, name)
    k.qkv_tiles = [f32t(f'qkv{i}') for i in range(12)]
    k.g_tiles = [f32t(f'g{i}') for i in range(4)]
    k.t_pool = Pool([f32t(f't{i}') for i in range(40)])
    k.n_pool = Pool([f32t(f'n{i}') for i in range(8)])
    k.x_pool = Pool([f32t(f'x{i}') for i in range(4)])
    k.r_pool = Pool([f32t(f'r{i}') for i in range(3)])
    k.mo_pool = Pool([f32t(f'mo{i}') for i in range(3)])
    k.rows_pool = Pool([ar.alloc([4, C], F32, f'rows{i}') for i in range(2)])
    k.cols_pool = Pool([ar.alloc([24], F32, f'cols{i}') for i in range(2)])
    k.dg_pool = Pool([ar.alloc([4], F32, f'dg{i}') for i in range(2)])
    k.bc_pool = Pool([ar.alloc([512], F32, f'bc{i}') for i in range(6)])
    k.SA = [f32t(f'SA{i}') for i in range(4)]
    k.mA_tiles = {}
    for h in range(4):
        P.op('dve', lambda e, h=h: e.memset(k.SA[h][:, :], 0.0), writes=[k.SA[h]])
    print("mixer A arena KB", ar.peak * 4 / 1024, ar.off * 4 / 1024)
    pass_a(P, k, uT, wA, mA_d)
    P.barrier()
    ar.reset(base2)
    wB = ar.alloc([8, NB], BF16, 'wB')
    for dc in range(8):
        P.op('pool', lambda e, dc=dc: e.dma_start(out=wB[:, dc, :], in_=wB_d[:, dc, :]), writes=[wB], dma=True)
    k.lw = ar.alloc([2, 512], F32, 'lw')
    k.lg = ar.alloc([2, 512], F32, 'lg')
    P.op('sp', lambda e: e.dma_start(out=k.lw[0:64, :, :], in_=lw_d), writes=[k.lw], dma=True)
    P.op('sp', lambda e: e.dma_start(out=k.lg[:, :, :], in_=lg_d), writes=[k.lg], dma=True)
    k.qkv_tiles = [f32t(f'rkv{i}') for i in range(12)]
    k.g_tiles = [f32t(f'gb{i}') for i in range(4)]
    k.lor_tiles = [f32t(f'lor{i}') for i in range(4)]
    k.t_pool = Pool([f32t(f'tb{i}') for i in range(56)])
    k.n_pool = Pool([f32t(f'nb{i}') for i in range(8)])
    k.x_pool = Pool([f32t(f'xb{i}') for i in range(4)])
    k.r_pool = Pool([f32t(f'rb{i}') for i in range(3)])
    k.ma_pool = Pool([f32t(f'ma{i}') for i in range(3)])
    k.mob_pool = Pool([ar.alloc([128], BF16, f'mob{i}') for i in range(3)])
    k.wc_pool = Pool([ar.alloc([4], F32, f'wc{i}') for i in range(8)])
    k.SB = [ar.alloc([64], F32, f'SB{i}') for i in range(4)]
    for i in range(4):
        P.op('dve', lambda e, i=i: e.memset(k.SB[i][:, :], 0.0), writes=[k.SB[i]])
    k.mrg_ts = {(i, hf): [] for i in range(4) for hf in range(2)}
    def new_mrg(i, hf):
        t = Tile(None, 'mrg')
        k.mrg_ts[(i, hf)].append(t)
        return t
    k.new_mrg = new_mrg
    print("mixer B arena KB", ar.peak * 4 / 1024, ar.off * 4 / 1024)
    pass_b(P, k, uT, wB, mA_d, mrg_d)
    mall_t = {}
    for i in range(4):
        for hf in range(2):
            mall_t[(i, hf)] = Tile(None, 'mall')
            P.op('pool', lambda e, i=i, hf=hf: e.collective_compute("AllGather", ALU.bypass, replica_groups=rg, ins=[mrg_d[i][hf].opt()], outs=[mrg_all[i][hf].opt()]),
                 reads=k.mrg_ts[(i, hf)], writes=[mall_t[(i, hf)]], cc=True)
    P.barrier()
    ar.reset(base0)
    hT = ar.alloc([8, TH], F32, 'hT2')
    assert ar.off == base1
    P.op('sp', lambda e: e.dma_start(out=hT[:, :, :], in_=h_sp.rearrange("(c p) t -> p c t", p=128)), reads=[hsp_t], writes=[hT], dma=True)
    wo = ar.alloc([8, D], BF16, 'wo')
    for dc in range(8):
        P.op('pool', lambda e, dc=dc: e.dma_start(out=wo[:, dc, :], in_=wo_d[:, dc, :]), writes=[wo], dma=True)
    mg_pool = Pool([ar.alloc([8, NT], BF16, f'mg{i}') for i in range(2)])
    lo_pool = Pool([ar.alloc([8, NT], BF16, f'lo{i}') for i in range(4)])
    for j in range(TH // NT):
        mg = mg_pool.get()
        lo = lo_pool.get()
        hi = lo_pool.get()
        for cc in range(8):
            r_, i_ = cc // 4, cc % 4
            P.op('sp', lambda e, lo=lo, j=j, cc=cc, r_=r_, i_=i_: e.dma_start(out=lo[:, cc, :], in_=mrg_all[i_][0][r_ * 128:(r_ + 1) * 128, j * NT:(j + 1) * NT]),
                 reads=[mall_t[(i_, 0)]], writes=[lo], dma=True)
            P.op('sp', lambda e, hi=hi, j=j, cc=cc, r_=r_, i_=i_: e.dma_start(out=hi[:, cc, :], in_=mrg_all[i_][1][r_ * 128:(r_ + 1) * 128, j * NT:(j + 1) * NT]),
                 reads=[mall_t[(i_, 1)]], writes=[hi], dma=True)
        TS(P, 'dve', lo, lo[:, :, :], lo, lo[:, :, :], k.par[:, PC_1MF:PC_1MF + 1], None, ALU.mult, rd=[k.par])
        STT(P, mg, mg[:, :, :], hi, hi[:, :, :], k.par[:, PC_F:PC_F + 1], lo, lo[:, :, :], ALU.mult, ALU.add, rd=[k.par])
        for dco in range(8):
            po = k.psum.get()
            for cc in range(8):
                MM(P, po, po[:, 0:NT], wo, wo[:, cc, dco * 128:(dco + 1) * 128], mg, mg[:, cc, :], cc == 0, cc == 7)
            TT(P, 'dve', hT, hT[:, dco, j * NT:(j + 1) * NT], po, po[:, 0:NT], hT, hT[:, dco, j * NT:(j + 1) * NT], ALU.add)
    P.barrier()
    ffn_alloc()
    ffn(P, k, hT, wgu2_d, wd2_d, PC_FFN2)
    P.barrier()
    ar.reset(base1)
    fin = [ar.alloc([8, NT], F32, f'fin{i}') for i in range(2)]
    k.sq_pool = Pool([ar.alloc([8, NT], BF16, f'sqc{i}') for i in range(2)])
    k.rstd_pool = Pool([ar.alloc([NT], F32, f'rstdc{i}') for i in range(2)])
    k.ntmp_pool = Pool([ar.alloc([NT], F32, f'ntmpc{i}') for i in range(2)])
    otiles = []
    for j in range(TH // NT):
        ft = fin[j % 2]
        rmsnorm_tile(P, k, hT, PC_FIN, ft, slice(0, NT), slice(j * NT, (j + 1) * NT), NT)
        od = Tile(None, f'od{j}')
        P.op('sp', lambda e, ft=ft, j=j: e.dma_start(out=out_d.rearrange("(c p) t -> p c t", p=128)[:, :, j * NT:(j + 1) * NT], in_=ft[:, :, :]),
             reads=[ft], writes=[od], dma=True)
        otiles.append(od)
    P.op('sp', lambda e: None, reads=otiles, noop=True)
    return nc, P, k


IN_OFF = dict(aq=0, ak=1024, av=2048, az=3072, ab=4096, aa=4104, bc=4112, ga=7472, gb=8496)


def consts():
    cst = np.zeros((128, NCONST), np.float32)
    r = np.arange(128)[:, None]
    c = np.arange(128)[None, :]
    cst[:, CC_ID:CC_ID + 128] = (r == c)
    cst[:, CC_SL:CC_SL + 128] = (r > c)
    cst[:, CC_SU:CC_SU + 128] = (r < c)
    cst[:, CC_IU:CC_IU + 128] = (r <= c)
    cst[:, CC_ONE:CC_ONE + 128] = 1.0
    cst[:, CC_BLK:CC_BLK + 128] = ((r // 64) == (c // 64))
    sel = np.zeros((4, 644), np.float32)
    for h in range(4):
        sel[h, h * 128:(h + 1) * 128] = 1.0
    sel[:, 512:516] = np.eye(4)
    sel[:, 516:644] = 1.0
    return cst, sel


def ffn_w(wgu, wd):
    g = wgu[:, :DFF].reshape(8, 128, NFC, 128)
    u = wgu[:, DFF:].reshape(8, 128, NFC, 128)
    wgu_r = np.ascontiguousarray(np.concatenate([g, u], axis=3).transpose(2, 1, 0, 3))
    return wgu_r, np.ascontiguousarray(wd.reshape(NFC, 128, D))


def host_prep(inputs, c, shared):
    b, hh = c // 2, c % 2
    seq = np.zeros((TP, D), np.float32)
    seq[112:128] = inputs["meta_tokens"]
    seq[128:] = inputs["x"][b]
    xT = np.ascontiguousarray(seq[hh * TH:(hh + 1) * TH].T)
    par = np.zeros((128, NPAR), np.float32)
    col8 = lambda v: v.reshape(8, 128).T
    par[:, PC_FFN1:PC_FFN1 + 8] = col8(inputs["ffn1_norm"][0])
    par[:, PC_MIX:PC_MIX + 8] = col8(inputs["mix_norm"][0])
    par[:, PC_FFN2:PC_FFN2 + 8] = col8(inputs["ffn2_norm"][0])
    par[:, PC_FIN:PC_FIN + 8] = col8(inputs["final_norm"])
    cw = inputs["a_conv_w"][0]
    for i in range(12):
        grp, hi = i // 4, i % 4
        base = grp * 1024 + (4 * hh + hi) * 128
        par[:, PC_CONV + i * 4:PC_CONV + i * 4 + 4] = cw[:, base:base + 128].T
    par[:, PC_OG] = inputs["a_out_norm"][0]
    mu = inputs["b_shift_mu"][0]
    h0 = 512 * hh
    for i in range(12):
        grp, ti = i // 4, i % 4
        par[:, PC_MU + i] = mu[grp * 1024 + h0 + ti * 128: grp * 1024 + h0 + (ti + 1) * 128]
    par[0:64, PC_MUL + 0] = mu[3072:3136]
    par[0:64, PC_MUL + 1] = mu[3136:3200]
    par[0:128, PC_MUL + 2] = mu[3200:3328]
    par[0:32, PC_MUL + 3] = mu[3328:3360]
    for name, pc in (("b_w0", PC_W0), ("b_a0", PC_A0), ("b_k_k", PC_KK), ("b_k_a", PC_KA), ("b_ln_gain", PC_LNG), ("b_ln_bias", PC_LNB)):
        v = inputs[name][0][h0:h0 + 512]
        par[:, pc:pc + 4] = v.reshape(4, 128).T
    par[:, PC_RK:PC_RK + 4] = inputs["b_r_k"][0].reshape(-1)[h0:h0 + 512].reshape(4, 128).T
    par[:, PC_F] = hh
    par[:, PC_1MF] = 1 - hh
    hp = np.zeros((4, 4), np.float32)
    hp[:, 0] = inputs["a_log_rate"][0][4 * hh:4 * hh + 4]
    hp[:, 1] = inputs["a_dt_bias"][0][4 * hh:4 * hh + 4]
    w_in = inputs["w_in"][0]
    O = IN_OFF
    hs = slice(h0, h0 + 512)
    colsA = np.concatenate([w_in[:, O['aq']:O['aq'] + 1024][:, hs], w_in[:, O['ak']:O['ak'] + 1024][:, hs],
                            w_in[:, O['av']:O['av'] + 1024][:, hs], w_in[:, O['az']:O['az'] + 1024][:, hs],
                            w_in[:, O['ga']:O['ga'] + 1024][:, hs],
                            w_in[:, O['ab'] + 4 * hh:O['ab'] + 4 * hh + 4], w_in[:, O['aa'] + 4 * hh:O['aa'] + 4 * hh + 4]], axis=1)
    assert colsA.shape[1] == NA
    bc = w_in[:, O['bc']:O['bc'] + 3360]
    colsB = np.concatenate([bc[:, 0:1024][:, hs], bc[:, 1024:2048][:, hs], bc[:, 2048:3072][:, hs],
                            w_in[:, O['gb']:O['gb'] + 1024][:, hs], bc[:, 3072:3360]], axis=1)
    assert colsB.shape[1] == NB
    r3 = lambda w: np.ascontiguousarray(w.reshape(8, 128, -1).transpose(1, 0, 2))
    lw = np.zeros((64, 2, 512), np.float32)
    lw[:, 0] = inputs["b_w_up"][0][:, hs]
    lw[:, 1] = inputs["b_a_up"][0][:, hs]
    lg = np.zeros((128, 2, 512), np.float32)
    lg[:, 0] = inputs["b_g_up"][0][0:128, hs]
    lg[0:32, 1] = inputs["b_g_up"][0][128:160, hs]
    d = dict(xT=xT, par=par, hp=hp, wA=r3(colsA), wB=r3(colsB), lw=lw, lg=lg)
    d.update(shared)
    return d


def make_in_maps(inputs):
    cst, sel = consts()
    wgu1, wd1 = ffn_w(inputs["ffn1_w_gu"][0], inputs["ffn1_w_down"][0])
    wgu2, wd2 = ffn_w(inputs["ffn2_w_gu"][0], inputs["ffn2_w_down"][0])
    wo = np.ascontiguousarray(inputs["w_out"][0].reshape(8, 128, D).transpose(1, 0, 2))
    shared = dict(cst=cst, sel=sel, wgu1=wgu1, wd1=wd1, wgu2=wgu2, wd2=wd2, wo=wo)
    return [host_prep(inputs, c, shared) for c in range(8)]


def assemble(results):
    out = np.zeros((4, 4096, D), np.float32)
    for b in range(4):
        full = np.concatenate([results[2 * b]["outT"].T, results[2 * b + 1]["outT"].T], axis=0)
        out[b] = full[128:]
    return out


from concourse.bass_utils import run_bass_kernel_spmd


def kernel(**inputs):
    inputs = {kk: np.asarray(v) for kk, v in inputs.items()}
    nc, P, k = build()
    with contextlib.ExitStack() as es:
        block = es.enter_context(nc.Block())
        P.finalize(block, lambda name: es.enter_context(nc.semaphore(name)))
    in_maps = make_in_maps(inputs)
    res = run_bass_kernel_spmd(nc, in_maps, core_ids=list(range(8)))
    return assemble(res.results)
```

```python
import contextlib
import numpy as np
import concourse.bass as bass
import concourse.mybir as mybir

F32 = mybir.dt.float32
BF16 = mybir.dt.bfloat16
ALU = mybir.AluOpType
AF = mybir.ActivationFunctionType
ENGS = ['pe', 'act', 'dve', 'pool', 'sp']
EPOCH = 2000
NSLOT = 8


class Tile:
    __slots__ = ('ap', 'w', 'rs', 'name')

    def __init__(self, ap, name=''):
        self.ap = ap
        self.w = None
        self.rs = []
        self.name = name

    def __getitem__(self, k):
        return self.ap[k]


class Op:
    __slots__ = ('eng', 'fn', 'waits', 'dwaits', 'idx', 'signal', 'isdma', 'slot', 'semval',
                 'clock', 'signo', 'inc', 'noop')


class Prog:
    def __init__(self, nc):
        self.nc = nc
        self.ops = {e: [] for e in ENGS}
        self.known = {e: {f: -1 for f in ENGS} for e in ENGS}
        self.kdma = {e: {} for e in ENGS}
        self.slots = {q: [[None, 0, None] for _ in range(NSLOT)] for q in ('sp', 'pool')}
        self.slot_rr = {'sp': 0, 'pool': 0}
        self.ccs = []

    def op(self, eng, fn, reads=(), writes=(), dma=False, cc=False, noop=False):
        o = Op()
        o.noop = noop
        o.eng = eng
        o.fn = fn
        o.idx = len(self.ops[eng])
        o.signal = False
        o.isdma = dma or cc
        o.slot = None
        o.semval = 0
        o.inc = 16
        deps = []
        rawset = set()
        for t in reads:
            if t.w is not None:
                deps.append(t.w)
                rawset.add(id(t.w))
        for t in writes:
            if t.w is not None:
                deps.append(t.w)
            deps.extend(t.rs)
        waits = {}
        dwaits = []
        if o.isdma:
            if cc:
                slot = ('cc', len(self.ccs))
                self.ccs.append(o)
                o.slot = slot
                o.semval = 1
                o.inc = 1
            else:
                k = self.slot_rr[eng]
                self.slot_rr[eng] = (k + 1) % NSLOT
                sl = self.slots[eng][k]
                prev = sl[2]
                if prev is not None:
                    deps.append(prev)
                sl[1] += 1
                sl[2] = o
                o.slot = (eng, k)
                o.semval = 16 * sl[1]
        seen = set()
        for d in deps:
            if d is o or id(d) in seen:
                continue
            seen.add(id(d))
            if d.isdma:
                if self.kdma[eng].get(d.slot, 0) < d.semval:
                    dwaits.append(d)
                    self.kdma[eng][d.slot] = d.semval
            else:
                if d.eng == eng:
                    if eng == 'pe' or id(d) not in rawset:
                        continue
                if self.known[eng][d.eng] < d.idx:
                    waits[d.eng] = max(waits.get(d.eng, -1), d.idx)
        for f, j in waits.items():
            dop = self.ops[f][j]
            dop.signal = True
            for g, v in dop.clock.items():
                if self.known[eng][g] < v:
                    self.known[eng][g] = v
            if self.known[eng][f] < j:
                self.known[eng][f] = j
        o.waits = list(waits.items())
        o.dwaits = dwaits
        ck = dict(self.known[eng])
        if not o.isdma:
            ck[eng] = o.idx
        o.clock = ck
        for t in reads:
            t.rs.append(o)
        for t in writes:
            t.w = o
            t.rs = []
        self.ops[eng].append(o)
        return o

    def barrier(self):
        lasts = {e: (self.ops[e][-1] if self.ops[e] else None) for e in ENGS}
        dummy = Tile(None, 'bar')
        for e in ENGS:
            cnt = 0
            for o in reversed(self.ops[e]):
                if o.noop:
                    if e in ('sp', 'pool'):
                        break
                    continue
                dummy.rs.append(o)
                cnt += 1
                if (not o.isdma) or cnt >= 2 * NSLOT + 4:
                    break
        rs = list(dummy.rs)
        for e in ENGS:
            dummy.rs = list(rs)
            dummy.w = None
            self.op(e, lambda eng: None, writes=[dummy], noop=True)

    def finalize(self, block, sem_alloc):
        nc = self.nc
        nsig = {}
        for e in ENGS:
            n = 0
            for o in self.ops[e]:
                if o.signal:
                    o.signo = n
                    n += 1
            nsig[e] = n
        esems = {e: [sem_alloc(f"s_{e}_{i}") for i in range((nsig[e] + EPOCH - 1) // EPOCH + 1)] for e in ENGS}
        ssems = {}
        for q in ('sp', 'pool'):
            for k in range(NSLOT):
                ssems[(q, k)] = sem_alloc(f"d_{q}_{k}")
        for i in range(len(self.ccs)):
            ssems[('cc', i)] = sem_alloc(f"cc_{i}")
        ops = self.ops

        def emit(e, engname):
            for o in ops[engname]:
                for f, j in o.waits:
                    n = ops[f][j].signo
                    e.wait_ge(esems[f][n // EPOCH], n % EPOCH + 1)
                for d in o.dwaits:
                    e.wait_ge(ssems[d.slot], d.semval)
                ins = o.fn(e)
                if ins is None:
                    assert not o.signal and not o.isdma, (engname, o.idx)
                    continue
                if o.isdma:
                    if o.inc == 1:
                        ins.then_inc(ssems[o.slot])
                    else:
                        ins.then_inc(ssems[o.slot], 16)
                elif o.signal:
                    ins.then_inc(esems[engname][o.signo // EPOCH], 1)

        @block.tensor
        def _(e):
            emit(e, 'pe')

        @block.scalar
        def _(e):
            emit(e, 'act')

        @block.vector
        def _(e):
            emit(e, 'dve')

        @block.gpsimd
        def _(e):
            emit(e, 'pool')

        @block.sync
        def _(e):
            emit(e, 'sp')
        return {e: len(ops[e]) for e in ENGS}, nsig


D = 1024
DFF = 2816
NFC = 22
TH = 2112
TP = 4224
NT = 352
NST = 2
STW = TH // NST
EPS = 1e-6


class Arena:
    def __init__(self, nc, name, nbytes):
        self.t = nc.alloc_sbuf_tensor(name, [128, nbytes // 4], F32)
        self.cap = nbytes // 4
        self.off = 0
        self.peak = 0

    def reset(self, off=0):
        self.off = off

    def alloc(self, free_shape, dtype=F32, name=''):
        n = int(np.prod(free_shape))
        esz = 4 if dtype == F32 else 2
        n4 = (n * esz + 3) // 4
        n4 = (n4 + 7) // 8 * 8
        assert self.off + n4 <= self.cap, (name, self.off, n4, self.cap)
        ap = self.t[:, self.off:self.off + n4]
        self.off += n4
        self.peak = max(self.peak, self.off)
        if dtype != F32:
            ap = ap.bitcast(dtype)
        ap = ap[:, 0:n]
        if len(free_shape) == 2:
            ap = ap.rearrange("p (a b) -> p a b", a=free_shape[0])
        elif len(free_shape) == 3:
            ap = ap.rearrange("p (a b c) -> p a b c", a=free_shape[0], b=free_shape[1])
        return Tile(ap, name)


class Pool:
    def __init__(self, tiles):
        self.tiles = tiles
        self.i = 0

    def get(self):
        t = self.tiles[self.i]
        self.i = (self.i + 1) % len(self.tiles)
        return t


class K:
    pass


def rmsnorm_tile(P, k, hT, gain_col0, out_t, out_sl, tok_sl, n):
    sq = k.sq_pool.get()
    P.op('act', lambda e: e.activation(sq[:, :, 0:n], hT[:, :, tok_sl], AF.Square), reads=[hT], writes=[sq])
    ps = k.psum.get()
    for dc in range(8):
        P.op('pe', lambda e, dc=dc: e.matmul(ps[:, 0:n], k.ones_bf[:, :], sq[:, dc, 0:n], start=(dc == 0), stop=(dc == 7)),
             reads=[sq, k.ones_bf], writes=[ps])
    rstd = k.rstd_pool.get()
    P.op('act', lambda e: e.activation(rstd[:, 0:n], ps[:, 0:n], AF.Sqrt, bias=k.epsc[:, 1:2]), reads=[ps, k.epsc], writes=[rstd])
    P.op('dve', lambda e: e.reciprocal(rstd[:, 0:n], rstd[:, 0:n]), reads=[rstd], writes=[rstd])
    for dc in range(8):
        if dc % 2 == 0:
            P.op('dve', lambda e, dc=dc: e.scalar_tensor_tensor(out_t[:, dc, out_sl], hT[:, dc, tok_sl],
                                                                k.par[:, gain_col0 + dc:gain_col0 + dc + 1],
                                                                rstd[:, 0:n], ALU.mult, ALU.mult),
                 reads=[hT, rstd, k.par], writes=[out_t])
        else:
            tmp = k.ntmp_pool.get()
            P.op('pool', lambda e, dc=dc, tmp=tmp: e.tensor_scalar(tmp[:, 0:n], hT[:, dc, tok_sl],
                                                                   k.par[:, gain_col0 + dc:gain_col0 + dc + 1], None, ALU.mult),
                 reads=[hT, k.par], writes=[tmp])
            P.op('pool', lambda e, dc=dc, tmp=tmp: e.tensor_tensor(out_t[:, dc, out_sl], tmp[:, 0:n], rstd[:, 0:n], ALU.mult),
                 reads=[tmp, rstd], writes=[out_t])


def ffn(P, k, hT, wgu_d, wd_d, gain_col0):
    nt_per = STW // NT
    for st in range(NST):
        hn = k.hn
        for j in range(nt_per):
            t0 = st * STW + j * NT
            rmsnorm_tile(P, k, hT, gain_col0, hn, slice(j * NT, (j + 1) * NT), slice(t0, t0 + NT), NT)
        for fc in range(NFC):
            wg = k.wgu_pool.get()
            P.op('pool', lambda e, wg=wg, fc=fc: e.dma_start(out=wg[:, :, :], in_=wgu_d[fc]), writes=[wg], dma=True)
            wdn = k.wd_pool.tiles[fc]
            if st == 0:
                P.op('pool', lambda e, wdn=wdn, fc=fc: e.dma_start(out=wdn[:, :], in_=wd_d[fc]), writes=[wdn], dma=True)
            for j in range(nt_per):
                tsl = slice(j * NT, (j + 1) * NT)
                pg = k.psum.get()
                for dc in range(8):
                    P.op('pe', lambda e, dc=dc, pg=pg, wg=wg, tsl=tsl: e.matmul(pg[:, 0:NT], wg[:, dc, 0:128], hn[:, dc, tsl], start=(dc == 0), stop=(dc == 7)),
                         reads=[wg, hn], writes=[pg])
                pu = k.psum.get()
                for dc in range(8):
                    P.op('pe', lambda e, dc=dc, pu=pu, wg=wg, tsl=tsl: e.matmul(pu[:, 0:NT], wg[:, dc, 128:256], hn[:, dc, tsl], start=(dc == 0), stop=(dc == 7)),
                         reads=[wg, hn], writes=[pu])
                sg = k.sg_pool.get()
                P.op('act', lambda e, sg=sg, pg=pg: e.activation(sg[:, 0:NT], pg[:, 0:NT], AF.Silu), reads=[pg], writes=[sg])
                at = k.act_tiles[fc]
                P.op('dve', lambda e, sg=sg, pu=pu, at=at, tsl=tsl: e.tensor_tensor(at[:, tsl], sg[:, 0:NT], pu[:, 0:NT], ALU.mult),
                     reads=[sg, pu], writes=[at])
            k.wd_res[fc] = wdn
        for j in range(nt_per):
            tsl = slice(j * NT, (j + 1) * NT)
            t0 = st * STW + j * NT
            for dco in range(8):
                po = k.psum.get()
                for fc in range(NFC):
                    P.op('pe', lambda e, fc=fc, po=po, dco=dco, tsl=tsl: e.matmul(po[:, 0:NT], k.wd_res[fc][:, dco * 128:(dco + 1) * 128], k.act_tiles[fc][:, tsl],
                                                                                   start=(fc == 0), stop=(fc == NFC - 1)),
                         reads=[k.wd_res[fc], k.act_tiles[fc]], writes=[po])
                P.op('dve', lambda e, po=po, dco=dco, t0=t0: e.scalar_tensor_tensor(hT[:, dco, t0:t0 + NT], po[:, 0:NT], 0.5, hT[:, dco, t0:t0 + NT], ALU.mult, ALU.add),
                     reads=[po, hT], writes=[hT])


C = 128
NCH = TP // C
HALO = 3
NA = 2568
NB = 2336
GN_EPS = 64 * 1e-5
PC_FFN1, PC_MIX, PC_FFN2, PC_FIN = 0, 8, 16, 24
PC_CONV = 32
PC_OG = 80
PC_MU = 81
PC_OMM = 93
PC_W0H = 105
PC_A0H = 109
PC_KK = 113
PC_KA = 117
PC_RK = 121
PC_LNG = 125
PC_LNB = 129
PC_MUL = 133
PC_OMML = 137
PC_W0, PC_A0 = 141, 145
PC_OGH = 149
PC_F, PC_1MF = 150, 151
NPAR = 160
CC_ID, CC_SL, CC_SU, CC_IU, CC_ONE, CC_BLK = 0, 128, 256, 384, 512, 640
NCONST = 768


def MM(P, o, oap, l, lap, r, rap, st=True, sp=True):
    P.op('pe', lambda e: e.matmul(oap, lap, rap, start=st, stop=sp), reads=[l, r], writes=[o])


def TT(P, eng, o, oap, a, aap, b, bap, op):
    P.op(eng, lambda e: e.tensor_tensor(oap, aap, bap, op), reads=[a, b], writes=[o])


def TS(P, eng, o, oap, a, aap, s1, s2, op0, op1=None, rd=()):
    if op1 is None:
        P.op(eng, lambda e: e.tensor_scalar(oap, aap, s1, None, op0), reads=[a] + list(rd), writes=[o])
    else:
        P.op(eng, lambda e: e.tensor_scalar(oap, aap, s1, s2, op0, op1), reads=[a] + list(rd), writes=[o])


def STT(P, o, oap, a, aap, sc, b, bap, op0, op1, rd=()):
    P.op('dve', lambda e: e.scalar_tensor_tensor(oap, aap, sc, bap, op0, op1), reads=[a, b] + list(rd), writes=[o])


def ACTF(P, o, oap, a, aap, func, bias=None, scale=None, rd=()):
    kw = {}
    if bias is not None:
        kw['bias'] = bias
    if scale is not None:
        kw['scale'] = scale
    P.op('act', lambda e: e.activation(oap, aap, func, **kw), reads=[a] + list(rd), writes=[o])


def tri_inverse(P, k, NT0, N0):
    cst = k.cst
    X = k.x_pool.get()
    TT(P, 'dve', X, X[:, :], N0, N0[:, :], cst, cst[:, CC_ID:CC_ID + 128], ALU.add)
    Ncur, NTcur = N0, NT0
    for lev in range(1, 7):
        last = (lev == 6)
        pnt = k.psum.get()
        MM(P, pnt, pnt[:, 0:128], Ncur, Ncur[:, :], NTcur, NTcur[:, :])
        if not last:
            pn = k.psum.get()
            MM(P, pn, pn[:, 0:128], NTcur, NTcur[:, :], Ncur, Ncur[:, :])
        NTn = k.n_pool.get()
        P.op('act', lambda e, NTn=NTn, pnt=pnt: e.copy(NTn[:, :], pnt[:, 0:128]), reads=[pnt], writes=[NTn])
        if not last:
            Nn = k.n_pool.get()
            P.op('act', lambda e, Nn=Nn, pn=pn: e.copy(Nn[:, :], pn[:, 0:128]), reads=[pn], writes=[Nn])
        px = k.psum.get()
        MM(P, px, px[:, 0:128], NTn, NTn[:, :], X, X[:, :])
        Xn = k.x_pool.get()
        TT(P, 'dve', Xn, Xn[:, :], px, px[:, 0:128], X, X[:, :], ALU.add)
        X = Xn
        NTcur = NTn
        if not last:
            Ncur = Nn
    return X


def capture(gen, box):
    box['v'] = yield from gen


def run_rr(gens):
    gens = list(gens)
    while gens:
        for g in list(gens):
            try:
                next(g)
            except StopIteration:
                gens.remove(g)


def tri_inverse_g(P, k, heads):
    cst = k.cst
    st = []
    for (NT0, N0, npool, xpool) in heads:
        X = xpool.get()
        TT(P, 'dve', X, X[:, :], N0, N0[:, :], cst, cst[:, CC_ID:CC_ID + 128], ALU.add)
        st.append([N0, NT0, X, npool, xpool, None])
    for lev in range(1, 7):
        last = (lev == 6)
        for hst in st:
            Ncur, NTcur, X, npool, xpool, _ = hst
            pnt = k.psum.get()
            MM(P, pnt, pnt[:, 0:128], Ncur, Ncur[:, :], NTcur, NTcur[:, :])
            if not last:
                pn = k.psum.get()
                MM(P, pn, pn[:, 0:128], NTcur, NTcur[:, :], Ncur, Ncur[:, :])
            NTn = npool.get()
            P.op('act', lambda e, NTn=NTn, pnt=pnt: e.copy(NTn[:, :], pnt[:, 0:128]), reads=[pnt], writes=[NTn])
            if not last:
                Nn = npool.get()
                P.op('act', lambda e, Nn=Nn, pn=pn: e.copy(Nn[:, :], pn[:, 0:128]), reads=[pn], writes=[Nn])
                hst[0] = Nn
            hst[1] = NTn
        yield
        for hst in st:
            Ncur, NTn, X, npool, xpool, _ = hst
            px = k.psum.get()
            MM(P, px, px[:, 0:128], NTn, NTn[:, :], X, X[:, :])
            Xn = xpool.get()
            TT(P, 'dve', Xn, Xn[:, :], px, px[:, 0:128], X, X[:, :], ALU.add)
            hst[2] = Xn
        yield
    return [hst[2] for hst in st]


def rstd_from_ps(P, k, ps, n, eps_col, scale, parts=128):
    r = k.r_pool.get()
    ACTF(P, r, r[0:parts, 0:n], ps, ps[0:parts, 0:n], AF.Sqrt, bias=k.epsc[0:parts, eps_col:eps_col + 1], scale=scale, rd=[k.epsc])
    P.op('dve', lambda e: e.reciprocal(r[0:parts, 0:n], r[0:parts, 0:n]), reads=[r], writes=[r])
    return r


def pass_a(P, k, uT, wA, mA_d):
    cst, par = k.cst, k.par
    import os

    def prolog(c):
        t0 = c * C
        qkv = []
        for i in range(12):
            ps = k.psum.get()
            for dc in range(8):
                MM(P, ps, ps[:, 0:C + HALO], wA, wA[:, dc, i * 128:(i + 1) * 128], uT, uT[:, dc, t0:t0 + C + HALO], dc == 0, dc == 7)
            acc = k.t_pool.get()
            w0 = PC_CONV + i * 4
            TS(P, 'dve', acc, acc[:, :], ps, ps[:, 0:C], par[:, w0:w0 + 1], None, ALU.mult, rd=[par])
            for j in range(1, 4):
                STT(P, acc, acc[:, :], ps, ps[:, j:j + C], par[:, w0 + j:w0 + j + 1], acc, acc[:, :], ALU.mult, ALU.add, rd=[par])
            t = k.qkv_sets[c % 2][i]
            ACTF(P, t, t[:, :], acc, acc[:, :], AF.Silu)
            qkv.append(t)
            yield
        for i in range(8):
            sq = k.t_pool.get()
            ACTF(P, sq, sq[:, :], qkv[i], qkv[i][:, :], AF.Square)
            ps = k.psum.get()
            MM(P, ps, ps[:, 0:C], cst, cst[:, CC_ONE:CC_ONE + 128], sq, sq[:, :])
            r = rstd_from_ps(P, k, ps, C, 0, 1.0)
            STT(P, qkv[i], qkv[i][:, :], qkv[i], qkv[i][:, :], (128 ** -0.5) if i < 4 else 1.0, r, r[:, 0:C], ALU.mult, ALU.mult)
            yield
        G = []
        for i in range(4):
            ps = k.psum.get()
            for dc in range(8):
                MM(P, ps, ps[:, 0:C], wA, wA[:, dc, 1536 + i * 128:1536 + (i + 1) * 128], uT, uT[:, dc, t0 + HALO:t0 + HALO + C], dc == 0, dc == 7)
            zs = k.t_pool.get()
            ACTF(P, zs, zs[:, :], ps, ps[:, 0:C], AF.Silu)
            ps2 = k.psum.get()
            for dc in range(8):
                MM(P, ps2, ps2[:, 0:C], wA, wA[:, dc, 2048 + i * 128:2048 + (i + 1) * 128], uT, uT[:, dc, t0 + HALO:t0 + HALO + C], dc == 0, dc == 7)
            th = k.t_pool.get()
            ACTF(P, th, th[:, :], ps2, ps2[:, 0:C], AF.Tanh, scale=0.5)
            g = k.g_sets[c % 2][i]
            STT(P, g, g[:, :], th, th[:, :], 1.0, zs, zs[:, :], ALU.add, ALU.mult)
            G.append(g)
            yield
        psb = k.psum.get()
        for dc in range(8):
            MM(P, psb, psb[0:4, 0:C], wA, wA[:, dc, 2560:2564], uT, uT[:, dc, t0 + HALO:t0 + HALO + C], dc == 0, dc == 7)
        psa = k.psum.get()
        for dc in range(8):
            MM(P, psa, psa[0:4, 0:C], wA, wA[:, dc, 2564:2568], uT, uT[:, dc, t0 + HALO:t0 + HALO + C], dc == 0, dc == 7)
        rows = k.rows_pool.get()
        ACTF(P, rows, rows[0:4, 0, :], psb, psb[0:4, 0:C], AF.Tanh, scale=0.5)
        TS(P, 'dve', rows, rows[0:4, 0, :], rows, rows[0:4, 0, :], 0.5, 0.5, ALU.mult, ALU.add)
        ACTF(P, rows, rows[0:4, 3, :], psa, psa[0:4, 0:C], AF.Exp, bias=k.hp[0:4, 1:2], rd=[k.hp])
        ACTF(P, rows, rows[0:4, 3, :], rows, rows[0:4, 3, :], AF.Ln, bias=k.epsc[0:4, 3:4], rd=[k.epsc])
        TS(P, 'dve', rows, rows[0:4, 1, :], rows, rows[0:4, 3, :], k.hp[0:4, 2:3], None, ALU.mult, rd=[k.hp])
        P.op('dve', lambda e, rows=rows: e.tensor_tensor_scan(rows[0:4, 2, :], k.onesrow[0:4, 0:C], rows[0:4, 1, :], 0.0, ALU.mult, ALU.add),
             reads=[rows, k.onesrow], writes=[rows])
        yield
        pc = k.psum.get()
        MM(P, pc, pc[:, 0:4], rows, rows[0:4, 2, :], k.sel, k.sel[0:4, 512:516])
        MM(P, pc, pc[:, 4:8], rows, rows[0:4, 0, :], k.sel, k.sel[0:4, 512:516])
        dg = k.dg_pool.get()
        TS(P, 'dve', dg, dg[0:4, 0:4], k.sel, k.sel[0:4, 512:516], rows[0:4, 2, C - 1:C], None, ALU.mult, rd=[rows])
        MM(P, pc, pc[:, 8:12], k.sel, k.sel[0:4, 516:644], dg, dg[0:4, 0:4])
        cols = k.cols_pool.get()
        P.op('act', lambda e, cols=cols, pc=pc: e.copy(cols[:, 0:12], pc[:, 0:12]), reads=[pc], writes=[cols])
        ACTF(P, cols, cols[:, 12:16], cols, cols[:, 0:4], AF.Exp)
        TT(P, 'dve', cols, cols[:, 12:16], cols, cols[:, 12:16], cols, cols[:, 4:8], ALU.mult)
        TT(P, 'dve', cols, cols[:, 16:20], cols, cols[:, 8:12], cols, cols[:, 0:4], ALU.subtract)
        ACTF(P, cols, cols[:, 16:20], cols, cols[:, 16:20], AF.Exp)
        ACTF(P, cols, cols[:, 20:24], cols, cols[:, 8:12], AF.Exp)
        yield
        pgr = k.psum.get()
        pbr = k.psum.get()
        for h in range(4):
            MM(P, pgr, pgr[:, h * 128:(h + 1) * 128], k.sel, k.sel[0:4, h * 128:(h + 1) * 128], rows, rows[0:4, 2, :])
            MM(P, pbr, pbr[:, h * 128:(h + 1) * 128], k.sel, k.sel[0:4, h * 128:(h + 1) * 128], rows, rows[0:4, 0, :])
        grow = k.bc_pool.get()
        brow = k.bc_pool.get()
        egrow = k.bc_pool.get()
        P.op('act', lambda e, grow=grow, pgr=pgr: e.copy(grow[:, :], pgr[:, :]), reads=[pgr], writes=[grow])
        P.op('act', lambda e, brow=brow, pbr=pbr: e.copy(brow[:, :], pbr[:, :]), reads=[pbr], writes=[brow])
        ACTF(P, egrow, egrow[:, :], pgr, pgr[:, :], AF.Exp)
        return dict(qkv=qkv, G=G, cols=cols, grow=grow, brow=brow, egrow=egrow, t0=t0, c=c)

    def make_heads(ctx):
        qkv, G, cols, grow, brow, egrow, t0, c = (ctx[x] for x in ('qkv', 'G', 'cols', 'grow', 'brow', 'egrow', 't0', 'c'))

        def head_a(h):
            sl = k.hslots[h]
            tp, npool, xpool = sl['tp'], sl['np'], sl['xp']
            hs = slice(h * 128, (h + 1) * 128)
            qn, kn, vs = qkv[h], qkv[4 + h], qkv[8 + h]
            kbT = tp.get()
            TT(P, 'pool', kbT, kbT[:, :], kn, kn[:, :], brow, brow[:, hs], ALU.mult)
            d1 = tp.get()
            TS(P, 'dve', d1, d1[:, :], grow, grow[:, hs], cols[:, h:h + 1], 0.0, ALU.subtract, ALU.max, rd=[cols])
            ACTF(P, d1, d1[:, :], d1, d1[:, :], AF.Exp, scale=-1.0)
            d2 = tp.get()
            TS(P, 'dve', d2, d2[:, :], grow, grow[:, hs], cols[:, h:h + 1], 0.0, ALU.subtract, ALU.min, rd=[cols])
            ACTF(P, d2, d2[:, :], d2, d2[:, :], AF.Exp)
            gm = tp.get()
            TT(P, 'pool', gm, gm[:, :], d1, d1[:, :], cst, cst[:, CC_SL:CC_SL + 128], ALU.mult)
            gmt = tp.get()
            TT(P, 'pool', gmt, gmt[:, :], d2, d2[:, :], cst, cst[:, CC_SU:CC_SU + 128], ALU.mult)
            git = tp.get()
            TT(P, 'pool', git, git[:, :], d2, d2[:, :], cst, cst[:, CC_IU:CC_IU + 128], ALU.mult)
            pA = k.psum.get()
            MM(P, pA, pA[:, 0:128], kbT, kbT[:, :], kn, kn[:, :])
            pAT = k.psum.get()
            MM(P, pAT, pAT[:, 0:128], kn, kn[:, :], kbT, kbT[:, :])
            NT0 = npool.get()
            STT(P, NT0, NT0[:, :], pA, pA[:, 0:128], -1.0, gm, gm[:, :], ALU.mult, ALU.mult)
            N0 = npool.get()
            STT(P, N0, N0[:, :], pAT, pAT[:, 0:128], -1.0, gmt, gmt[:, :], ALU.mult, ALU.mult)
            yield
            X = (yield from tri_inverse_g(P, k, [(NT0, N0, npool, xpool)]))[0]
            pat = k.psum.get()
            MM(P, pat, pat[:, 0:128], kn, kn[:, :], qn, qn[:, :])
            attnT = tp.get()
            TT(P, 'dve', attnT, attnT[:, :], pat, pat[:, 0:128], git, git[:, :], ALU.mult)
            pk = k.psum.get()
            MM(P, pk, pk[:, 0:128], kn, kn[:, :], cst, cst[:, CC_ID:CC_ID + 128])
            kbg = tp.get()
            TS(P, 'dve', kbg, kbg[:, :], pk, pk[:, 0:128], cols[:, 12 + h:13 + h], None, ALU.mult, rd=[cols])
            kt = tp.get()
            TS(P, 'dve', kt, kt[:, :], pk, pk[:, 0:128], cols[:, 16 + h:17 + h], None, ALU.mult, rd=[cols])
            pv = k.psum.get()
            MM(P, pv, pv[:, 0:128], vs, vs[:, :], cst, cst[:, CC_ID:CC_ID + 128])
            vb = tp.get()
            TS(P, 'dve', vb, vb[:, :], pv, pv[:, 0:128], cols[:, 4 + h:5 + h], None, ALU.mult, rd=[cols])
            qgT = tp.get()
            TT(P, 'pool', qgT, qgT[:, :], qn, qn[:, :], egrow, egrow[:, hs], ALU.mult)
            pu = k.psum.get()
            MM(P, pu, pu[:, 0:128], X, X[:, :], vb, vb[:, :])
            u = tp.get()
            P.op('act', lambda e, u=u, pu=pu: e.copy(u[:, :], pu[:, 0:128]), reads=[pu], writes=[u])
            pw = k.psum.get()
            MM(P, pw, pw[:, 0:128], kbg, kbg[:, :], X, X[:, :])
            wkT = tp.get()
            P.op('act', lambda e, wkT=wkT, pw=pw: e.copy(wkT[:, :], pw[:, 0:128]), reads=[pw], writes=[wkT])
            yield
            S = k.SA[h]
            p1 = k.psum.get()
            MM(P, p1, p1[:, 0:128], wkT, wkT[:, :], S, S[:, :])
            vnew = tp.get()
            TT(P, 'dve', vnew, vnew[:, :], u, u[:, :], p1, p1[:, 0:128], ALU.subtract)
            yield
            po = k.psum.get()
            MM(P, po, po[:, 0:128], S, S[:, :], qgT, qgT[:, :], True, False)
            MM(P, po, po[:, 0:128], vnew, vnew[:, :], attnT, attnT[:, :], False, True)
            pS = k.psum.get()
            MM(P, pS, pS[:, 0:128], kt, kt[:, :], vnew, vnew[:, :])
            STT(P, S, S[:, :], S, S[:, :], cols[:, 20 + h:21 + h], pS, pS[:, 0:128], ALU.mult, ALU.add, rd=[cols])
            osq = tp.get()
            ACTF(P, osq, osq[:, :], po, po[:, 0:128], AF.Square)
            pn = k.psum.get()
            MM(P, pn, pn[:, 0:128], cst, cst[:, CC_ONE:CC_ONE + 128], osq, osq[:, :])
            r = rstd_from_ps(P, k, pn, C, 1, 1.0 / 128)
            om = tp.get()
            TT(P, 'dve', om, om[:, :], po, po[:, 0:128], r, r[:, 0:C], ALU.mult)
            mo = k.mo_pool.get()
            STT(P, mo, mo[:, :], om, om[:, :], par[:, PC_OGH:PC_OGH + 1], G[h], G[h][:, :], ALU.mult, ALU.mult, rd=[par])
            dt_ = Tile(None, 'mAd')
            k.mA_tiles[(c, h)] = dt_
            P.op('sp', lambda e, mo=mo, h=h, t0=t0: e.dma_start(out=mA_d[h * 128:(h + 1) * 128, t0:t0 + C], in_=mo[:, :]),
                 reads=[mo], writes=[dt_], dma=True)


        return [head_a(h) for h in range(4)]

    n = int(os.environ.get('NCHA', NCH))
    box = {}
    if n > 0:
        run_rr([capture(prolog(0), box)])
    for c in range(n):
        gens = make_heads(box['v'])
        if c + 1 < n:
            gens.append(capture(prolog(c + 1), box))
        run_rr(gens)


def pass_b(P, k, uT, wB, mA_d, mrg_d, dbg_out=None):
    cst, par = k.cst, k.par
    H1 = 1
    import os

    def prolog(c):
        t0 = c * C
        o0 = t0 + HALO - H1
        rkv = []
        for i in range(12):
            ps = k.psum.get()
            for dc in range(8):
                MM(P, ps, ps[:, 0:C + 1], wB, wB[:, dc, i * 128:(i + 1) * 128], uT, uT[:, dc, o0:o0 + C + 1], dc == 0, dc == 7)
            tm = k.t_pool.get()
            TS(P, 'dve', tm, tm[:, :], ps, ps[:, 0:C], par[:, PC_MU + i:PC_MU + i + 1], None, ALU.mult, rd=[par])
            t = k.qkv_sets[c % 2][i]
            STT(P, t, t[:, :], ps, ps[:, 1:C + 1], par[:, PC_OMM + i:PC_OMM + i + 1], tm, tm[:, :], ALU.mult, ALU.add, rd=[par])
            rkv.append(t)
            yield
        GB = []
        for i in range(4):
            ps = k.psum.get()
            for dc in range(8):
                MM(P, ps, ps[:, 0:C], wB, wB[:, dc, 1536 + i * 128:1536 + (i + 1) * 128], uT, uT[:, dc, t0 + HALO:t0 + HALO + C], dc == 0, dc == 7)
            g = k.g_sets[c % 2][i]
            ACTF(P, g, g[:, :], ps, ps[:, 0:C], AF.Tanh, scale=0.5)
            GB.append(g)
            yield
        lor = []
        for j, (c0, n) in enumerate([(2048, 64), (2112, 64), (2176, 128), (2304, 32)]):
            ps = k.psum.get()
            for dc in range(8):
                MM(P, ps, ps[0:n, 0:C + 1], wB, wB[:, dc, c0:c0 + n], uT, uT[:, dc, o0:o0 + C + 1], dc == 0, dc == 7)
            tm = k.t_pool.get()
            TS(P, 'dve', tm, tm[0:n, :], ps, ps[0:n, 0:C], par[0:n, PC_MUL + j:PC_MUL + j + 1], None, ALU.mult, rd=[par])
            t = k.lor_sets[c % 2][j]
            STT(P, t, t[0:n, :], ps, ps[0:n, 1:C + 1], par[0:n, PC_OMML + j:PC_OMML + j + 1], tm, tm[0:n, :], ALU.mult, ALU.add, rd=[par])
            lor.append(t)
            yield
        ACTF(P, lor[0], lor[0][0:64, :], lor[0], lor[0][0:64, :], AF.Tanh)
        for j, n in ((2, 128), (3, 32)):
            ACTF(P, lor[j], lor[j][0:n, :], lor[j], lor[j][0:n, :], AF.Tanh, scale=0.5)
            TS(P, 'dve', lor[j], lor[j][0:n, :], lor[j], lor[j][0:n, :], 1.0, 0.5, ALU.add, ALU.mult)
        return dict(rkv=rkv, GB=GB, lor=lor, t0=t0, c=c)

    def make_pairs(ctx):
        rkv, GB, lor, t0, c = (ctx[x] for x in ('rkv', 'GB', 'lor', 't0', 'c'))

        def pair_b(i, slot):
            sl = k.pslots[slot]
            tp = sl['tp']
            cs = slice(i * 128, (i + 1) * 128)
            rf, kf, vf = rkv[i], rkv[4 + i], rkv[8 + i]
            ps = k.psum.get()
            MM(P, ps, ps[:, 0:C], k.lw, k.lw[0:64, 0, cs], lor[0], lor[0][0:64, :])
            lw = tp.get()
            ACTF(P, lw, lw[:, :], ps, ps[:, 0:C], AF.Tanh, bias=par[:, PC_W0H + i:PC_W0H + i + 1], scale=0.5, rd=[par])
            TS(P, 'pool', lw, lw[:, :], lw, lw[:, :], 1.0, -0.5 * float(np.exp(-0.5)), ALU.add, ALU.mult)
            ps = k.psum.get()
            MM(P, ps, ps[:, 0:C], k.lw, k.lw[0:64, 1, cs], lor[1], lor[1][0:64, :])
            asig = tp.get()
            ACTF(P, asig, asig[:, :], ps, ps[:, 0:C], AF.Tanh, bias=par[:, PC_A0H + i:PC_A0H + i + 1], scale=0.5, rd=[par])
            TS(P, 'pool', asig, asig[:, :], asig, asig[:, :], 1.0, 0.5, ALU.add, ALU.mult)
            psg = k.psum.get()
            MM(P, psg, psg[:, 0:C], k.lg, k.lg[:, 0, cs], lor[2], lor[2][:, :], True, False)
            MM(P, psg, psg[:, 0:C], k.lg, k.lg[0:32, 1, cs], lor[3], lor[3][0:32, :], False, True)
            gate = tp.get()
            STT(P, gate, gate[:, :], GB[i], GB[i][:, :], 1.0, psg, psg[:, 0:C], ALU.add, ALU.mult)
            kk = tp.get()
            TS(P, 'pool', kk, kk[:, :], kf, kf[:, :], par[:, PC_KK + i:PC_KK + i + 1], None, ALU.mult, rd=[par])
            sq = tp.get()
            ACTF(P, sq, sq[:, :], kk, kk[:, :], AF.Square)
            ps = k.psum.get()
            MM(P, ps, ps[:, 0:C], cst, cst[:, CC_BLK:CC_BLK + 128], sq, sq[:, :])
            r = rstd_from_ps(P, k, ps, C, 0, 1.0)
            TT(P, 'pool', kk, kk[:, :], kk, kk[:, :], r, r[:, 0:C], ALU.mult)
            km = tp.get()
            TS(P, 'pool', km, km[:, :], asig, asig[:, :], -1.0, None, ALU.add)
            TS(P, 'pool', km, km[:, :], km, km[:, :], par[:, PC_KA + i:PC_KA + i + 1], None, ALU.mult, rd=[par])
            STT(P, km, km[:, :], km, km[:, :], 1.0, kf, kf[:, :], ALU.add, ALU.mult)
            pr = tp.get()
            STT(P, pr, pr[:, :], rf, rf[:, :], par[:, PC_RK + i:PC_RK + i + 1], km, km[:, :], ALU.mult, ALU.mult, rd=[par])
            prk = k.psum.get()
            MM(P, prk, prk[:, 0:C], cst, cst[:, CC_BLK:CC_BLK + 128], pr, pr[:, :])
            rkv_t = tp.get()
            TT(P, 'dve', rkv_t, rkv_t[:, :], prk, prk[:, 0:C], vf, vf[:, :], ALU.mult)
            yield
            cumI = tp.get()
            P.op('dve', lambda e, cumI=cumI, lw=lw: e.tensor_tensor_scan(cumI[:, :], k.onesrow[:, 0:C], lw[:, :], 0.0, ALU.mult, ALU.add),
                 reads=[lw, k.onesrow], writes=[cumI])
            eE = tp.get()
            TT(P, 'pool', eE, eE[:, :], cumI, cumI[:, :], lw, lw[:, :], ALU.subtract)
            ACTF(P, eE, eE[:, :], eE, eE[:, :], AF.Exp)
            eI = tp.get()
            ACTF(P, eI, eI[:, :], cumI, cumI[:, :], AF.Exp)
            eN = tp.get()
            ACTF(P, eN, eN[:, :], cumI, cumI[:, :], AF.Exp, scale=-1.0)
            atil = tp.get()
            STT(P, atil, atil[:, :], kk, kk[:, :], -1.0, eE, eE[:, :], ALU.mult, ALU.mult)
            rtil = tp.get()
            TT(P, 'pool', rtil, rtil[:, :], rf, rf[:, :], eI, eI[:, :], ALU.mult)
            btil = tp.get()
            TT(P, 'pool', btil, btil[:, :], kk, kk[:, :], asig, asig[:, :], ALU.mult)
            TT(P, 'pool', btil, btil[:, :], btil, btil[:, :], eN, eN[:, :], ALU.mult)
            ktil = tp.get()
            TT(P, 'pool', ktil, ktil[:, :], km, km[:, :], eN, eN[:, :], ALU.mult)
            yield
            bh = tp.get()
            TS(P, 'pool', bh, bh[:, :], btil, btil[:, :], eI[:, C - 1:C], None, ALU.mult, rd=[eI])
            kh = tp.get()
            TS(P, 'pool', kh, kh[:, :], ktil, ktil[:, :], eI[:, C - 1:C], None, ALU.mult, rd=[eI])
            wc = k.wc_pool.get()
            P.op('act', lambda e, wc=wc, eI=eI: e.copy(wc[:, 0:1], eI[:, C - 1:C]), reads=[eI], writes=[wc])
            tms = []
            for src in (bh, kh, vf):
                pt = k.psum.get()
                MM(P, pt, pt[:, 0:128], src, src[:, :], cst, cst[:, CC_ID:CC_ID + 128])
                tmt = tp.get()
                P.op('act', lambda e, tmt=tmt, pt=pt: e.copy(tmt[:, :], pt[:, 0:128]), reads=[pt], writes=[tmt])
                tms.append(tmt)
            Bh, Kh, Vt = tms
            ST = k.SB[i]
            Pp = tp.get()
            Ysb = tp.get()
            yield
            hd = []
            for j in range(2):
                rs = slice(64 * j, 64 * j + 64)
                pab = k.psum.get()
                MM(P, pab, pab[:, 0:128], atil, atil[rs, :], btil, btil[rs, :])
                pabT = k.psum.get()
                MM(P, pabT, pabT[:, 0:128], btil, btil[rs, :], atil, atil[rs, :])
                NT0 = sl['np'][j].get()
                TT(P, 'dve', NT0, NT0[:, :], pab, pab[:, 0:128], cst, cst[:, CC_SL:CC_SL + 128], ALU.mult)
                N0 = sl['np'][j].get()
                TT(P, 'dve', N0, N0[:, :], pabT, pabT[:, 0:128], cst, cst[:, CC_SU:CC_SU + 128], ALU.mult)
                hd.append((NT0, N0, sl['np'][j], sl['xp'][j]))
            yield
            Xs = yield from tri_inverse_g(P, k, hd)
            allm = []
            for j in range(2):
                rs = slice(64 * j, 64 * j + 64)
                mats = []
                for (la, ra, mk) in ((ktil, atil, CC_SU), (btil, rtil, CC_IU), (ktil, rtil, CC_IU)):
                    pm = k.psum.get()
                    MM(P, pm, pm[:, 0:128], la, la[rs, :], ra, ra[rs, :])
                    mt = tp.get()
                    TT(P, 'dve', mt, mt[:, :], pm, pm[:, 0:128], cst, cst[:, mk:mk + 128], ALU.mult)
                    mats.append(mt)
                allm.append(mats)
            yield
            Gss = []
            for j in range(2):
                rs = slice(64 * j, 64 * j + 64)
                AakT, ArbT, ArkT = allm[j]
                pG = k.psum.get()
                MM(P, pG, pG[:, 0:64], atil, atil[rs, :], ST, ST[rs, :], True, False)
                MM(P, pG, pG[:, 0:64], AakT, AakT[:, :], Vt, Vt[:, rs], False, True)
                Gs = tp.get()
                P.op('act', lambda e, Gs=Gs, pG=pG: e.copy(Gs[:, 0:64], pG[:, 0:64]), reads=[pG], writes=[Gs])
                Gss.append(Gs)
            yield
            for j in range(2):
                rs = slice(64 * j, 64 * j + 64)
                pP = k.psum.get()
                MM(P, pP, pP[:, 0:64], Xs[j], Xs[j][:, :], Gss[j], Gss[j][:, 0:64])
                P.op('act', lambda e, Pp=Pp, pP=pP, rs=rs: e.copy(Pp[:, rs], pP[:, 0:64]), reads=[pP], writes=[Pp])
            yield
            for j in range(2):
                rs = slice(64 * j, 64 * j + 64)
                AakT, ArbT, ArkT = allm[j]
                pY = k.psum.get()
                MM(P, pY, pY[:, 0:64], rtil, rtil[rs, :], ST, ST[rs, :], True, False)
                MM(P, pY, pY[:, 0:64], ArbT, ArbT[:, :], Pp, Pp[:, rs], False, False)
                MM(P, pY, pY[:, 0:64], ArkT, ArkT[:, :], Vt, Vt[:, rs], False, True)
                P.op('act', lambda e, Ysb=Ysb, pY=pY, rs=rs: e.copy(Ysb[:, rs], pY[:, 0:64]), reads=[pY], writes=[Ysb])
            pS = k.psum.get()
            MM(P, pS, pS[:, 0:128], Bh, Bh[:, :], Pp, Pp[:, :], True, False)
            MM(P, pS, pS[:, 0:128], Kh, Kh[:, :], Vt, Vt[:, :], False, True)
            for j in range(2):
                rs = slice(64 * j, 64 * j + 64)
                STT(P, ST, ST[rs, :], ST, ST[rs, :], wc[rs, 0:1], pS, pS[rs, 64 * j:64 * j + 64], ALU.mult, ALU.add, rd=[wc])
            yield
            pyt = k.psum.get()
            MM(P, pyt, pyt[:, 0:128], Ysb, Ysb[:, :], cst, cst[:, CC_ID:CC_ID + 128])
            yT = tp.get()
            P.op('act', lambda e, yT=yT, pyt=pyt: e.copy(yT[:, :], pyt[:, 0:128]), reads=[pyt], writes=[yT])
            pmn = k.psum.get()
            MM(P, pmn, pmn[:, 0:C], cst, cst[:, CC_BLK:CC_BLK + 128], yT, yT[:, :])
            yc = tp.get()
            STT(P, yc, yc[:, :], pmn, pmn[:, 0:C], -1.0 / 64, yT, yT[:, :], ALU.mult, ALU.add)
            ysq = tp.get()
            ACTF(P, ysq, ysq[:, :], yc, yc[:, :], AF.Square)
            pvr = k.psum.get()
            MM(P, pvr, pvr[:, 0:C], cst, cst[:, CC_BLK:CC_BLK + 128], ysq, ysq[:, :])
            r = rstd_from_ps(P, k, pvr, C, 2, 1.0 / 64)
            TT(P, 'pool', yc, yc[:, :], yc, yc[:, :], r, r[:, 0:C], ALU.mult)
            TS(P, 'pool', yc, yc[:, :], yc, yc[:, :], par[:, PC_LNG + i:PC_LNG + i + 1], par[:, PC_LNB + i:PC_LNB + i + 1], ALU.mult, ALU.add, rd=[par])
            TT(P, 'pool', yc, yc[:, :], yc, yc[:, :], rkv_t, rkv_t[:, :], ALU.add)
            mb = tp.get()
            STT(P, mb, mb[:, :], yc, yc[:, :], 0.5, gate, gate[:, :], ALU.mult, ALU.mult)
            if dbg_out is not None:
                import os
                nm = os.environ.get('DBGB', 'mb')
                src = dict(mb=mb, rf=rf, kf=kf, vf=vf, lw=lw, asig=asig, gate=gate, kk=kk, km=km, yT=yT, yc=yc, rkv_t=rkv_t, atil=atil, rtil=rtil, btil=btil, ktil=ktil, cumI=cumI)[nm]
                P.op('sp', lambda e, src=src, i=i, t0=t0: e.dma_start(out=dbg_out[i * 128:(i + 1) * 128, t0:t0 + C], in_=src[:, :]),
                     reads=[src], writes=[k.new_mrg(i, 0)], dma=True)
                return
            ma = k.ma_pool.get()
            P.op('sp', lambda e, ma=ma, i=i, t0=t0: e.dma_start(out=ma[:, :], in_=mA_d[i * 128:(i + 1) * 128, t0:t0 + C]),
                 reads=[k.mA_tiles[(c, i)]] if (c, i) in k.mA_tiles else [], writes=[ma], dma=True)
            mo = k.mob_pool.get()
            TT(P, 'pool', mo, mo[:, :], mb, mb[:, :], ma, ma[:, :], ALU.add)
            segs = []
            if t0 + C <= TH:
                segs.append((0, t0, 0, C))
            elif t0 >= TH:
                segs.append((1, t0 - TH, 0, C))
            else:
                n0 = TH - t0
                segs.append((0, t0, 0, n0))
                segs.append((1, 0, n0, C - n0))
            for (hf, d0, s0, n) in segs:
                P.op('sp', lambda e, mo=mo, i=i, hf=hf, d0=d0, s0=s0, n=n: e.dma_start(out=mrg_d[i][hf][:, d0:d0 + n], in_=mo[:, s0:s0 + n]),
                     reads=[mo], writes=[k.new_mrg(i, hf)], dma=True)


        return pair_b

    n = int(os.environ.get('NCHB', NCH))
    box = {}
    if n > 0:
        run_rr([capture(prolog(0), box)])
    for c in range(n):
        pair_b = make_pairs(box['v'])
        extra = [capture(prolog(c + 1), box)] if c + 1 < n else []
        run_rr([pair_b(0, 0), pair_b(1, 1)] + extra)
        run_rr([pair_b(2, 0), pair_b(3, 1)] + extra)


def build(stage=9):
    nc = bass.Bass("TRN2", target_bir_lowering=False)
    k = K()
    dt = lambda name, shape, dty=F32, kind="ExternalInput": nc.dram_tensor(name, shape, dty, kind=kind).ap()
    xT_d = dt("xT", [D, TH])
    par_d = dt("par", [128, NPAR])
    cst_d = dt("cst", [128, NCONST])
    sel_d = dt("sel", [4, 644])
    hp_d = dt("hp", [4, 4])
    wgu1_d = dt("wgu1", [NFC, 128, 8, 256])
    wd1_d = dt("wd1", [NFC, 128, D])
    wgu2_d = dt("wgu2", [NFC, 128, 8, 256])
    wd2_d = dt("wd2", [NFC, 128, D])
    wA_d = dt("wA", [128, 8, NA])
    wB_d = dt("wB", [128, 8, NB])
    lw_d = dt("lw", [64, 2, 512])
    lg_d = dt("lg", [128, 2, 512])
    wo_d = dt("wo", [128, 8, D])
    out_d = dt("outT", [D, TH], F32, "ExternalOutput")
    u_src = [nc.dram_tensor(f"u_src{i}", [128, TH], BF16).ap() for i in range(8)]
    u_all = [nc.dram_tensor(f"u_all{i}", [256, TH], BF16).ap() for i in range(8)]
    h_sp = nc.dram_tensor("h_sp", [D, TH], F32).ap()
    mA_d = nc.dram_tensor("mA", [512, TP], F32).ap()
    mrg_d = [[nc.dram_tensor(f"mrg{i}_{hf}", [128, TH], BF16).ap() for hf in range(2)] for i in range(4)]
    mrg_all = [[nc.dram_tensor(f"mrga{i}_{hf}", [256, TH], BF16).ap() for hf in range(2)] for i in range(4)]
    rg = [[0, 1], [2, 3], [4, 5], [6, 7]]

    P = Prog(nc)
    ar = Arena(nc, "arena", 206 * 1024)
    k.psum = Pool([Tile(nc.alloc_psum_tensor(f"ps{i}", [128, 512], F32).ap(), f"ps{i}") for i in range(8)])
    k.par = ar.alloc([NPAR], F32, 'par')
    k.cst = ar.alloc([NCONST], F32, 'cst')
    k.ones_bf = ar.alloc([128], BF16, 'ones_bf')
    k.epsc = ar.alloc([4], F32, 'epsc')
    k.sel = ar.alloc([644], F32, 'sel')
    k.hp = ar.alloc([4], F32, 'hp')
    k.onesrow = ar.alloc([128], F32, 'onesrow')
    base0 = ar.off
    hT = ar.alloc([8, TH], F32, 'hT')
    base1 = ar.off

    def ffn_alloc():
        ar.reset(base1)
        k.hn = ar.alloc([8, STW], BF16, 'hn')
        k.act_tiles = [ar.alloc([STW], BF16, f'act{i}') for i in range(NFC)]
        k.wgu_pool = Pool([ar.alloc([8, 256], BF16, f'wgu{i}') for i in range(2)])
        k.wd_res = [None] * NFC
        k.wd_pool = Pool([ar.alloc([D], BF16, f'wd{i}') for i in range(NFC)])
        k.sq_pool = Pool([ar.alloc([8, NT], BF16, f'sq{i}') for i in range(1)])
        k.rstd_pool = Pool([ar.alloc([NT], F32, f'rstd{i}') for i in range(2)])
        k.sg_pool = Pool([ar.alloc([NT], F32, f'sg{i}') for i in range(2)])
        k.ntmp_pool = Pool([ar.alloc([NT], F32, f'ntmp{i}') for i in range(2)])

    P.op('sp', lambda e: e.dma_start(out=hT[:, :, :], in_=xT_d.rearrange("(c p) t -> p c t", p=128)), writes=[hT], dma=True)
    P.op('sp', lambda e: e.dma_start(out=k.par[:, :], in_=par_d), writes=[k.par], dma=True)
    P.op('sp', lambda e: e.dma_start(out=k.cst[:, :], in_=cst_d), writes=[k.cst], dma=True)
    P.op('sp', lambda e: e.dma_start(out=k.sel[0:4, :], in_=sel_d), writes=[k.sel], dma=True)
    P.op('sp', lambda e: e.dma_start(out=k.hp[0:4, :], in_=hp_d), writes=[k.hp], dma=True)
    P.op('dve', lambda e: e.memset(k.ones_bf[:, :], 1.0 / D), writes=[k.ones_bf])
    P.op('dve', lambda e: e.memset(k.onesrow[:, :], 1.0), writes=[k.onesrow])
    P.op('dve', lambda e: e.memset(k.epsc[:, 0:1], 1e-6), writes=[k.epsc])
    P.op('dve', lambda e: e.memset(k.epsc[:, 1:2], EPS), writes=[k.epsc])
    P.op('dve', lambda e: e.memset(k.epsc[:, 2:3], GN_EPS), writes=[k.epsc])
    P.op('dve', lambda e: e.memset(k.epsc[:, 3:4], 1.0), writes=[k.epsc])
    k.cst_eps = k.epsc
    TS(P, 'dve', k.par, k.par[:, PC_W0H:PC_W0H + 8], k.par, k.par[:, PC_W0:PC_W0 + 8], 0.5, None, ALU.mult)
    TS(P, 'dve', k.par, k.par[:, PC_OGH:PC_OGH + 1], k.par, k.par[:, PC_OG:PC_OG + 1], 0.5, None, ALU.mult)
    TS(P, 'dve', k.par, k.par[:, PC_OMM:PC_OMM + 12], k.par, k.par[:, PC_MU:PC_MU + 12], -1.0, 1.0, ALU.mult, ALU.add)
    TS(P, 'dve', k.par, k.par[:, PC_OMML:PC_OMML + 4], k.par, k.par[:, PC_MUL:PC_MUL + 4], -1.0, 1.0, ALU.mult, ALU.add)
    ACTF(P, k.hp, k.hp[0:4, 2:3], k.hp, k.hp[0:4, 0:1], AF.Exp)
    TS(P, 'dve', k.hp, k.hp[0:4, 2:3], k.hp, k.hp[0:4, 2:3], -1.0, None, ALU.mult)

    ffn_alloc()
    ffn(P, k, hT, wgu1_d, wd1_d, PC_FFN1)
    P.barrier()
    ar.reset(base1)
    k.sq_pool = Pool([ar.alloc([8, NT], BF16, f'sqb{i}') for i in range(2)])
    k.rstd_pool = Pool([ar.alloc([NT], F32, f'rstdb{i}') for i in range(2)])
    k.ntmp_pool = Pool([ar.alloc([NT], F32, f'ntmpb{i}') for i in range(2)])
    un = ar.alloc([8, TH], BF16, 'un')
    for j in range(TH // NT):
        rmsnorm_tile(P, k, hT, PC_MIX, un, slice(j * NT, (j + 1) * NT), slice(j * NT, (j + 1) * NT), NT)
    usrc_t = [Tile(None, f'usrc{i}') for i in range(8)]
    for dc in range(8):
        P.op('sp', lambda e, dc=dc: e.dma_start(out=u_src[dc], in_=un[:, dc, :]), reads=[un], writes=[usrc_t[dc]], dma=True)
    hsp_t = Tile(None, 'hsp')
    P.op('sp', lambda e: e.dma_start(out=h_sp.rearrange("(c p) t -> p c t", p=128), in_=hT[:, :, :]), reads=[hT], writes=[hsp_t], dma=True)
    uall_t = [Tile(None, f'uall{i}') for i in range(8)]
    for dc in range(8):
        P.op('pool', lambda e, dc=dc: e.collective_compute("AllGather", ALU.bypass, replica_groups=rg, ins=[u_src[dc].opt()], outs=[u_all[dc].opt()]),
             reads=[usrc_t[dc]], writes=[uall_t[dc]], cc=True)
    P.barrier()
    ar.reset(base0)
    uT = ar.alloc([8, HALO + TP], BF16, 'uT')
    P.op('dve', lambda e: e.memset(uT[:, :, 0:HALO], 0.0), writes=[uT])
    for r_ in range(2):
        for dc in range(8):
            P.op('sp', lambda e, r_=r_, dc=dc: e.dma_start(out=uT[:, dc, HALO + r_ * TH:HALO + (r_ + 1) * TH],
                                                          in_=u_all[dc][r_ * 128:(r_ + 1) * 128, :]),
                 reads=[uall_t[dc]], writes=[uT], dma=True)
    base2 = ar.off
    wA = ar.alloc([8, NA], BF16, 'wA')
    for dc in range(8):
        P.op('pool', lambda e, dc=dc: e.dma_start(out=wA[:, dc, :], in_=wA_d[:, dc, :]), writes=[wA], dma=True)
    f32t = lambda name: ar.alloc([128], F32, name)
    k.qkv_sets = [[f32t(f'qkv{q}_{i}') for i in range(12)] for q in range(2)]
    k.g_sets = [[f32t(f'g{q}_{i}') for i in range(4)] for q in range(2)]
    k.t_pool = Pool([f32t(f't{i}') for i in range(8)])
    k.hslots = [dict(tp=Pool([f32t(f'ht{h}_{i}') for i in range(16)]), np=Pool([f32t(f'hn{h}_{i}') for i in range(6)]),
                     xp=Pool([f32t(f'hx{h}_{i}') for i in range(3)])) for h in range(4)]
    k.r_pool = Pool([f32t(f'r{i}') for i in range(3)])
    k.mo_pool = Pool([f32t(f'mo{i}') for i in range(3)])
    k.rows_pool = Pool([ar.alloc([4, C], F32, f'rows{i}') for i in range(2)])
    k.cols_pool = Pool([ar.alloc([24], F32, f'cols{i}') for i in range(2)])
    k.dg_pool = Pool([ar.alloc([4], F32, f'dg{i}') for i in range(2)])
    k.bc_pool = Pool([ar.alloc([512], F32, f'bc{i}') for i in range(6)])
    k.SA = [f32t(f'SA{i}') for i in range(4)]
    k.mA_tiles = {}
    for h in range(4):
        P.op('dve', lambda e, h=h: e.memset(k.SA[h][:, :], 0.0), writes=[k.SA[h]])
    print("mixer A arena KB", ar.peak * 4 / 1024, ar.off * 4 / 1024)
    pass_a(P, k, uT, wA, mA_d)
    P.barrier()
    ar.reset(base2)
    wB = ar.alloc([8, NB], BF16, 'wB')
    for dc in range(8):
        P.op('pool', lambda e, dc=dc: e.dma_start(out=wB[:, dc, :], in_=wB_d[:, dc, :]), writes=[wB], dma=True)
    k.lw = ar.alloc([2, 512], F32, 'lw')
    k.lg = ar.alloc([2, 512], F32, 'lg')
    P.op('sp', lambda e: e.dma_start(out=k.lw[0:64, :, :], in_=lw_d), writes=[k.lw], dma=True)
    P.op('sp', lambda e: e.dma_start(out=k.lg[:, :, :], in_=lg_d), writes=[k.lg], dma=True)
    k.qkv_sets = [[f32t(f'rkv{q}_{i}') for i in range(12)] for q in range(2)]
    k.g_sets = [[f32t(f'gb{q}_{i}') for i in range(4)] for q in range(2)]
    k.lor_sets = [[f32t(f'lor{q}_{i}') for i in range(4)] for q in range(2)]
    k.t_pool = Pool([f32t(f'tb{i}') for i in range(8)])
    k.pslots = [dict(tp=Pool([f32t(f'pt{q}_{i}') for i in range(36)]),
                     np=[Pool([f32t(f'pn{q}_{j}_{i}') for i in range(6)]) for j in range(2)],
                     xp=[Pool([f32t(f'px{q}_{j}_{i}') for i in range(3)]) for j in range(2)]) for q in range(2)]
    k.r_pool = Pool([f32t(f'rb{i}') for i in range(3)])
    k.ma_pool = Pool([f32t(f'ma{i}') for i in range(3)])
    k.mob_pool = Pool([ar.alloc([128], BF16, f'mob{i}') for i in range(3)])
    k.wc_pool = Pool([ar.alloc([4], F32, f'wc{i}') for i in range(8)])
    k.SB = [ar.alloc([64], F32, f'SB{i}') for i in range(4)]
    for i in range(4):
        P.op('dve', lambda e, i=i: e.memset(k.SB[i][:, :], 0.0), writes=[k.SB[i]])
    k.mrg_ts = {(i, hf): [] for i in range(4) for hf in range(2)}
    def new_mrg(i, hf):
        t = Tile(None, 'mrg')
        k.mrg_ts[(i, hf)].append(t)
        return t
    k.new_mrg = new_mrg
    print("mixer B arena KB", ar.peak * 4 / 1024, ar.off * 4 / 1024)
    pass_b(P, k, uT, wB, mA_d, mrg_d)
    mall_t = {}
    for i in range(4):
        for hf in range(2):
            mall_t[(i, hf)] = Tile(None, 'mall')
            P.op('pool', lambda e, i=i, hf=hf: e.collective_compute("AllGather", ALU.bypass, replica_groups=rg, ins=[mrg_d[i][hf].opt()], outs=[mrg_all[i][hf].opt()]),
                 reads=k.mrg_ts[(i, hf)], writes=[mall_t[(i, hf)]], cc=True)
    P.barrier()
    ar.reset(base0)
    hT = ar.alloc([8, TH], F32, 'hT2')
    assert ar.off == base1
    P.op('sp', lambda e: e.dma_start(out=hT[:, :, :], in_=h_sp.rearrange("(c p) t -> p c t", p=128)), reads=[hsp_t], writes=[hT], dma=True)
    wo = ar.alloc([8, D], BF16, 'wo')
    for dc in range(8):
        P.op('pool', lambda e, dc=dc: e.dma_start(out=wo[:, dc, :], in_=wo_d[:, dc, :]), writes=[wo], dma=True)
    mg_pool = Pool([ar.alloc([8, NT], BF16, f'mg{i}') for i in range(2)])
    lo_pool = Pool([ar.alloc([8, NT], BF16, f'lo{i}') for i in range(4)])
    for j in range(TH // NT):
        mg = mg_pool.get()
        lo = lo_pool.get()
        hi = lo_pool.get()
        for cc in range(8):
            r_, i_ = cc // 4, cc % 4
            P.op('sp', lambda e, lo=lo, j=j, cc=cc, r_=r_, i_=i_: e.dma_start(out=lo[:, cc, :], in_=mrg_all[i_][0][r_ * 128:(r_ + 1) * 128, j * NT:(j + 1) * NT]),
                 reads=[mall_t[(i_, 0)]], writes=[lo], dma=True)
            P.op('sp', lambda e, hi=hi, j=j, cc=cc, r_=r_, i_=i_: e.dma_start(out=hi[:, cc, :], in_=mrg_all[i_][1][r_ * 128:(r_ + 1) * 128, j * NT:(j + 1) * NT]),
                 reads=[mall_t[(i_, 1)]], writes=[hi], dma=True)
        TS(P, 'dve', lo, lo[:, :, :], lo, lo[:, :, :], k.par[:, PC_1MF:PC_1MF + 1], None, ALU.mult, rd=[k.par])
        STT(P, mg, mg[:, :, :], hi, hi[:, :, :], k.par[:, PC_F:PC_F + 1], lo, lo[:, :, :], ALU.mult, ALU.add, rd=[k.par])
        for dco in range(8):
            po = k.psum.get()
            for cc in range(8):
                MM(P, po, po[:, 0:NT], wo, wo[:, cc, dco * 128:(dco + 1) * 128], mg, mg[:, cc, :], cc == 0, cc == 7)
            TT(P, 'dve', hT, hT[:, dco, j * NT:(j + 1) * NT], po, po[:, 0:NT], hT, hT[:, dco, j * NT:(j + 1) * NT], ALU.add)
    P.barrier()
    ffn_alloc()
    ffn(P, k, hT, wgu2_d, wd2_d, PC_FFN2)
    P.barrier()
    ar.reset(base1)
    fin = [ar.alloc([8, NT], F32, f'fin{i}') for i in range(2)]
    k.sq_pool = Pool([ar.alloc([8, NT], BF16, f'sqc{i}') for i in range(2)])
    k.rstd_pool = Pool([ar.alloc([NT], F32, f'rstdc{i}') for i in range(2)])
    k.ntmp_pool = Pool([ar.alloc([NT], F32, f'ntmpc{i}') for i in range(2)])
    otiles = []
    for j in range(TH // NT):
        ft = fin[j % 2]
        rmsnorm_tile(P, k, hT, PC_FIN, ft, slice(0, NT), slice(j * NT, (j + 1) * NT), NT)
        od = Tile(None, f'od{j}')
        P.op('sp', lambda e, ft=ft, j=j: e.dma_start(out=out_d.rearrange("(c p) t -> p c t", p=128)[:, :, j * NT:(j + 1) * NT], in_=ft[:, :, :]),
             reads=[ft], writes=[od], dma=True)
        otiles.append(od)
    P.op('sp', lambda e: None, reads=otiles, noop=True)
    return nc, P, k


IN_OFF = dict(aq=0, ak=1024, av=2048, az=3072, ab=4096, aa=4104, bc=4112, ga=7472, gb=8496)


def consts():
    cst = np.zeros((128, NCONST), np.float32)
    r = np.arange(128)[:, None]
    c = np.arange(128)[None, :]
    cst[:, CC_ID:CC_ID + 128] = (r == c)
    cst[:, CC_SL:CC_SL + 128] = (r > c)
    cst[:, CC_SU:CC_SU + 128] = (r < c)
    cst[:, CC_IU:CC_IU + 128] = (r <= c)
    cst[:, CC_ONE:CC_ONE + 128] = 1.0
    cst[:, CC_BLK:CC_BLK + 128] = ((r // 64) == (c // 64))
    sel = np.zeros((4, 644), np.float32)
    for h in range(4):
        sel[h, h * 128:(h + 1) * 128] = 1.0
    sel[:, 512:516] = np.eye(4)
    sel[:, 516:644] = 1.0
    return cst, sel


def ffn_w(wgu, wd):
    g = wgu[:, :DFF].reshape(8, 128, NFC, 128)
    u = wgu[:, DFF:].reshape(8, 128, NFC, 128)
    wgu_r = np.ascontiguousarray(np.concatenate([g, u], axis=3).transpose(2, 1, 0, 3))
    return wgu_r, np.ascontiguousarray(wd.reshape(NFC, 128, D))


def host_prep(inputs, c, shared):
    b, hh = c // 2, c % 2
    seq = np.zeros((TP, D), np.float32)
    seq[112:128] = inputs["meta_tokens"]
    seq[128:] = inputs["x"][b]
    xT = np.ascontiguousarray(seq[hh * TH:(hh + 1) * TH].T)
    par = np.zeros((128, NPAR), np.float32)
    col8 = lambda v: v.reshape(8, 128).T
    par[:, PC_FFN1:PC_FFN1 + 8] = col8(inputs["ffn1_norm"][0])
    par[:, PC_MIX:PC_MIX + 8] = col8(inputs["mix_norm"][0])
    par[:, PC_FFN2:PC_FFN2 + 8] = col8(inputs["ffn2_norm"][0])
    par[:, PC_FIN:PC_FIN + 8] = col8(inputs["final_norm"])
    cw = inputs["a_conv_w"][0]
    for i in range(12):
        grp, hi = i // 4, i % 4
        base = grp * 1024 + (4 * hh + hi) * 128
        par[:, PC_CONV + i * 4:PC_CONV + i * 4 + 4] = cw[:, base:base + 128].T
    par[:, PC_OG] = inputs["a_out_norm"][0]
    mu = inputs["b_shift_mu"][0]
    h0 = 512 * hh
    for i in range(12):
        grp, ti = i // 4, i % 4
        par[:, PC_MU + i] = mu[grp * 1024 + h0 + ti * 128: grp * 1024 + h0 + (ti + 1) * 128]
    par[0:64, PC_MUL + 0] = mu[3072:3136]
    par[0:64, PC_MUL + 1] = mu[3136:3200]
    par[0:128, PC_MUL + 2] = mu[3200:3328]
    par[0:32, PC_MUL + 3] = mu[3328:3360]
    for name, pc in (("b_w0", PC_W0), ("b_a0", PC_A0), ("b_k_k", PC_KK), ("b_k_a", PC_KA), ("b_ln_gain", PC_LNG), ("b_ln_bias", PC_LNB)):
        v = inputs[name][0][h0:h0 + 512]
        par[:, pc:pc + 4] = v.reshape(4, 128).T
    par[:, PC_RK:PC_RK + 4] = inputs["b_r_k"][0].reshape(-1)[h0:h0 + 512].reshape(4, 128).T
    par[:, PC_F] = hh
    par[:, PC_1MF] = 1 - hh
    hp = np.zeros((4, 4), np.float32)
    hp[:, 0] = inputs["a_log_rate"][0][4 * hh:4 * hh + 4]
    hp[:, 1] = inputs["a_dt_bias"][0][4 * hh:4 * hh + 4]
    w_in = inputs["w_in"][0]
    O = IN_OFF
    hs = slice(h0, h0 + 512)
    colsA = np.concatenate([w_in[:, O['aq']:O['aq'] + 1024][:, hs], w_in[:, O['ak']:O['ak'] + 1024][:, hs],
                            w_in[:, O['av']:O['av'] + 1024][:, hs], w_in[:, O['az']:O['az'] + 1024][:, hs],
                            w_in[:, O['ga']:O['ga'] + 1024][:, hs],
                            w_in[:, O['ab'] + 4 * hh:O['ab'] + 4 * hh + 4], w_in[:, O['aa'] + 4 * hh:O['aa'] + 4 * hh + 4]], axis=1)
    assert colsA.shape[1] == NA
    bc = w_in[:, O['bc']:O['bc'] + 3360]
    colsB = np.concatenate([bc[:, 0:1024][:, hs], bc[:, 1024:2048][:, hs], bc[:, 2048:3072][:, hs],
                            w_in[:, O['gb']:O['gb'] + 1024][:, hs], bc[:, 3072:3360]], axis=1)
    assert colsB.shape[1] == NB
    r3 = lambda w: np.ascontiguousarray(w.reshape(8, 128, -1).transpose(1, 0, 2))
    lw = np.zeros((64, 2, 512), np.float32)
    lw[:, 0] = inputs["b_w_up"][0][:, hs]
    lw[:, 1] = inputs["b_a_up"][0][:, hs]
    lg = np.zeros((128, 2, 512), np.float32)
    lg[:, 0] = inputs["b_g_up"][0][0:128, hs]
    lg[0:32, 1] = inputs["b_g_up"][0][128:160, hs]
    d = dict(xT=xT, par=par, hp=hp, wA=r3(colsA), wB=r3(colsB), lw=lw, lg=lg)
    d.update(shared)
    return d


def make_in_maps(inputs):
    cst, sel = consts()
    wgu1, wd1 = ffn_w(inputs["ffn1_w_gu"][0], inputs["ffn1_w_down"][0])
    wgu2, wd2 = ffn_w(inputs["ffn2_w_gu"][0], inputs["ffn2_w_down"][0])
    wo = np.ascontiguousarray(inputs["w_out"][0].reshape(8, 128, D).transpose(1, 0, 2))
    shared = dict(cst=cst, sel=sel, wgu1=wgu1, wd1=wd1, wgu2=wgu2, wd2=wd2, wo=wo)
    return [host_prep(inputs, c, shared) for c in range(8)]


def assemble(results):
    out = np.zeros((4, 4096, D), np.float32)
    for b in range(4):
        full = np.concatenate([results[2 * b]["outT"].T, results[2 * b + 1]["outT"].T], axis=0)
        out[b] = full[128:]
    return out


from concourse.bass_utils import run_bass_kernel_spmd


def kernel(**inputs):
    inputs = {kk: np.asarray(v) for kk, v in inputs.items()}
    nc, P, k = build()
    with contextlib.ExitStack() as es:
        block = es.enter_context(nc.Block())
        P.finalize(block, lambda name: es.enter_context(nc.semaphore(name)))
    in_maps = make_in_maps(inputs)
    res = run_bass_kernel_spmd(nc, in_maps, core_ids=list(range(8)))
    return assemble(res.results)
```
